# Optimizing a Trainium2 kernel written in Bass

```python
import math
import jax, jax.numpy as jnp
from jax import lax
import numpy as np

D_MODEL = 1024
BATCH = 8
SEQ = 2048
DEPTH = 4
DEC_BATCH = 16
DEC_SEQ = 2048
PAST_LEN = 128

MEM_LEN = 256
D_FF = 2816
C_A = D_MODEL // 2
A_GROUPS = 4
A_GROUP_DIM = C_A // A_GROUPS
C_B = D_MODEL - C_A
CONV_WIDTH = 31
CONV_PAD = (CONV_WIDTH - 1) // 2
AB_IN = C_A + 2 * C_B
MLA_HEADS = 8
QK_NOPE = 128
QK_ROPE = 64
V_DIM = 128
Q_LORA = 512
KV_LORA = 256
MLA_IN = Q_LORA + KV_LORA + QK_ROPE
ROPE_BASE = 10000.0
Q_BLOCK = 128
XA_HEADS = 4
XA_HEAD_DIM = D_MODEL // XA_HEADS
N_AB = (DEPTH + 1) // 2
N_MLA = DEPTH // 2
EPS = 1e-6

kernel_name = "hybrid_fnet_conv_mla_encoder"


def rms_norm(x, g):
    xf = x.astype(jnp.float32)
    y = xf * lax.rsqrt(jnp.mean(xf * xf, axis=-1, keepdims=True) + EPS)
    return (y * g.astype(jnp.float32)).astype(x.dtype)


def layer_norm(x, g, b):
    xf = x.astype(jnp.float32)
    mu = jnp.mean(xf, axis=-1, keepdims=True)
    xc = xf - mu
    var = jnp.mean(xc * xc, axis=-1, keepdims=True)
    y = xc * lax.rsqrt(var + EPS) * g.astype(jnp.float32) + b.astype(jnp.float32)
    return y.astype(x.dtype)


def swiglu(h, w_gate, w_up, w_down):
    return (jax.nn.silu(h @ w_gate) * (h @ w_up)) @ w_down


def rope_tables(seq):
    inv_freq = 1.0 / (ROPE_BASE ** (jnp.arange(0, QK_ROPE, 2, dtype=jnp.float32) / QK_ROPE))
    ang = jnp.arange(seq, dtype=jnp.float32)[:, None] * inv_freq[None, :]
    return jnp.cos(ang), jnp.sin(ang)


def apply_rope(x, cos, sin):
    half = x.shape[-1] // 2
    x1 = x[..., :half].astype(jnp.float32)
    x2 = x[..., half:].astype(jnp.float32)
    return jnp.concatenate([x1 * cos - x2 * sin, x1 * sin + x2 * cos], axis=-1).astype(x.dtype)


def fourier_mix(u):
    B, S, _ = u.shape
    uh = u.reshape(B, S, A_GROUPS, A_GROUP_DIM).astype(jnp.float32)
    y = jnp.fft.fft2(uh, axes=(1, 3), norm="ortho").real
    return y.reshape(B, S, C_A).astype(u.dtype)


def conv_module(u, conv_w, conv_b, ln_g, ln_b):
    a, g = u[..., :C_B], u[..., C_B:]
    h = a * jax.nn.sigmoid(g)
    h = lax.conv_general_dilated(
        h, conv_w[:, None, :].astype(h.dtype), window_strides=(1,),
        padding=[(CONV_PAD, CONV_PAD)], dimension_numbers=("NWC", "WIO", "NWC"),
        feature_group_count=C_B) + conv_b
    return jax.nn.silu(layer_norm(h, ln_g, ln_b))


def mixer_ab(h, w_in, conv_w, conv_b, ln_g, ln_b, w_out):
    u = h @ w_in
    ya = fourier_mix(u[..., :C_A])
    yb = conv_module(u[..., C_A:], conv_w, conv_b, ln_g, ln_b)
    return jnp.concatenate([ya, yb], axis=-1) @ w_out


def blocked_attention(q, k, v, scale):
    B, S, H, Dq = q.shape
    nb = S // Q_BLOCK
    qb = q.reshape(B, nb, Q_BLOCK, H, Dq).transpose(1, 0, 2, 3, 4)

    def one_block(qblk):
        s = jnp.einsum("bqhd,bkhd->bhqk", qblk, k).astype(jnp.float32) * scale
        p = jax.nn.softmax(s, axis=-1).astype(v.dtype)
        return jnp.einsum("bhqk,bkhd->bqhd", p, v)

    o = lax.map(one_block, qb)
    return o.transpose(1, 0, 2, 3, 4).reshape(B, S, H, v.shape[-1])


def mla(h, w_in, q_norm, w_q_b, kv_norm, w_kv_b, w_out, cos, sin):
    B, S, _ = h.shape
    u = h @ w_in
    cq = u[..., :Q_LORA]
    ckv = u[..., Q_LORA:Q_LORA + KV_LORA]
    kr = u[..., Q_LORA + KV_LORA:]
    q = (rms_norm(cq, q_norm) @ w_q_b).reshape(B, S, MLA_HEADS, QK_NOPE + QK_ROPE)
    q = jnp.concatenate([q[..., :QK_NOPE],
                         apply_rope(q[..., QK_NOPE:], cos[None, :, None, :], sin[None, :, None, :])], axis=-1)
    kv = (rms_norm(ckv, kv_norm) @ w_kv_b).reshape(B, S, MLA_HEADS, QK_NOPE + V_DIM)
    k_nope, v = kv[..., :QK_NOPE], kv[..., QK_NOPE:]
    kr = apply_rope(kr, cos[None], sin[None])
    k = jnp.concatenate([k_nope, jnp.broadcast_to(kr[:, :, None, :], (B, S, MLA_HEADS, QK_ROPE))], axis=-1)
    o = blocked_attention(q, k, v, 1.0 / math.sqrt(QK_NOPE + QK_ROPE))
    return o.reshape(B, S, MLA_HEADS * V_DIM) @ w_out


def cross_attn(h, m, w_q, w_kv, w_o):
    B, S, _ = h.shape
    M = m.shape[1]
    q = (h @ w_q).reshape(B, S, XA_HEADS, XA_HEAD_DIM)
    kv = (m @ w_kv).reshape(B, M, 2, XA_HEADS, XA_HEAD_DIM)
    k, v = kv[:, :, 0], kv[:, :, 1]
    s = jnp.einsum("bshd,bmhd->bhsm", q, k).astype(jnp.float32) * (1.0 / math.sqrt(XA_HEAD_DIM))
    p = jax.nn.softmax(s, axis=-1).astype(v.dtype)
    o = jnp.einsum("bhsm,bmhd->bshd", p, v).reshape(B, S, D_MODEL)
    return o @ w_o


def trunk(x, mem, p):
    cos, sin = rope_tables(x.shape[1])
    for l in range(DEPTH):
        h = rms_norm(x, p["ffn1_norm"][l])
        x = x + 0.5 * swiglu(h, p["ffn1_w_gate"][l], p["ffn1_w_up"][l], p["ffn1_w_down"][l])
        h = rms_norm(x, p["mix_norm"][l])
        i = l // 2
        if l % 2 == 0:
            x = x + mixer_ab(h, p["ab_w_in"][i], p["ab_conv_w"][i], p["ab_conv_b"][i],
                             p["ab_conv_ln_g"][i], p["ab_conv_ln_b"][i], p["ab_w_out"][i])
        else:
            x = x + mla(h, p["mla_w_in"][i], p["mla_q_norm"][i], p["mla_w_q_b"][i],
                        p["mla_kv_norm"][i], p["mla_w_kv_b"][i], p["mla_w_out"][i], cos, sin)
        h = rms_norm(x, p["xattn_norm"][l])
        m = rms_norm(mem, p["mem_norm"][l])
        x = x + cross_attn(h, m, p["xattn_w_q"][l], p["xattn_w_kv"][l], p["xattn_w_o"][l])
        h = rms_norm(x, p["ffn2_norm"][l])
        x = x + 0.5 * swiglu(h, p["ffn2_w_gate"][l], p["ffn2_w_up"][l], p["ffn2_w_down"][l])
    return rms_norm(x, p["final_norm"])


def _w(k, shape, fan_in):
    return jax.random.normal(k, shape, jnp.float32) * (fan_in ** -0.5)


def _gain(k, shape):
    return 1.0 + 0.01 * jax.random.normal(k, shape, jnp.float32)


def _small(k, shape):
    return 0.01 * jax.random.normal(k, shape, jnp.float32)


def setup_inputs(seed: int = 0) -> dict:
    key = jax.random.key(seed)
    ks = jax.random.split(key, 32)
    f32 = jnp.float32
    return {
        "x_prompt": jax.random.normal(ks[0], (BATCH, SEQ, D_MODEL), f32),
        "x_sample": jax.random.normal(ks[1], (DEC_BATCH, DEC_SEQ, D_MODEL), f32),
        "mem_prompt": jax.random.normal(ks[2], (BATCH, MEM_LEN, D_MODEL), f32),
        "mem_sample": jax.random.normal(ks[3], (DEC_BATCH, MEM_LEN, D_MODEL), f32),
        "ffn1_norm": _gain(ks[4], (DEPTH, D_MODEL)),
        "ffn1_w_gate": _w(ks[5], (DEPTH, D_MODEL, D_FF), D_MODEL),
        "ffn1_w_up": _w(ks[6], (DEPTH, D_MODEL, D_FF), D_MODEL),
        "ffn1_w_down": _w(ks[7], (DEPTH, D_FF, D_MODEL), D_FF),
        "mix_norm": _gain(ks[8], (DEPTH, D_MODEL)),
        "xattn_norm": _gain(ks[9], (DEPTH, D_MODEL)),
        "mem_norm": _gain(ks[10], (DEPTH, D_MODEL)),
        "xattn_w_q": _w(ks[11], (DEPTH, D_MODEL, D_MODEL), D_MODEL),
        "xattn_w_kv": _w(ks[12], (DEPTH, D_MODEL, 2 * D_MODEL), D_MODEL),
        "xattn_w_o": _w(ks[13], (DEPTH, D_MODEL, D_MODEL), D_MODEL),
        "ffn2_norm": _gain(ks[14], (DEPTH, D_MODEL)),
        "ffn2_w_gate": _w(ks[15], (DEPTH, D_MODEL, D_FF), D_MODEL),
        "ffn2_w_up": _w(ks[16], (DEPTH, D_MODEL, D_FF), D_MODEL),
        "ffn2_w_down": _w(ks[17], (DEPTH, D_FF, D_MODEL), D_FF),
        "ab_w_in": _w(ks[18], (N_AB, D_MODEL, AB_IN), D_MODEL),
        "ab_conv_w": _w(ks[19], (N_AB, CONV_WIDTH, C_B), CONV_WIDTH),
        "ab_conv_b": _small(ks[20], (N_AB, C_B)),
        "ab_conv_ln_g": _gain(ks[21], (N_AB, C_B)),
        "ab_conv_ln_b": _small(ks[22], (N_AB, C_B)),
        "ab_w_out": _w(ks[23], (N_AB, C_A + C_B, D_MODEL), C_A + C_B),
        "mla_w_in": _w(ks[24], (N_MLA, D_MODEL, MLA_IN), D_MODEL),
        "mla_q_norm": _gain(ks[25], (N_MLA, Q_LORA)),
        "mla_w_q_b": _w(ks[26], (N_MLA, Q_LORA, MLA_HEADS * (QK_NOPE + QK_ROPE)), Q_LORA),
        "mla_kv_norm": _gain(ks[27], (N_MLA, KV_LORA)),
        "mla_w_kv_b": _w(ks[28], (N_MLA, KV_LORA, MLA_HEADS * (QK_NOPE + V_DIM)), KV_LORA),
        "mla_w_out": _w(ks[29], (N_MLA, MLA_HEADS * V_DIM, D_MODEL), MLA_HEADS * V_DIM),
        "final_norm": _gain(ks[30], (D_MODEL,)),
    }


def reference(x_prompt, x_sample, mem_prompt, mem_sample,
              ffn1_norm, ffn1_w_gate, ffn1_w_up, ffn1_w_down,
              mix_norm, xattn_norm, mem_norm, xattn_w_q, xattn_w_kv, xattn_w_o,
              ffn2_norm, ffn2_w_gate, ffn2_w_up, ffn2_w_down,
              ab_w_in, ab_conv_w, ab_conv_b, ab_conv_ln_g, ab_conv_ln_b, ab_w_out,
              mla_w_in, mla_q_norm, mla_w_q_b, mla_kv_norm, mla_w_kv_b, mla_w_out,
              final_norm):
    p = {
        "ffn1_norm": ffn1_norm, "ffn1_w_gate": ffn1_w_gate, "ffn1_w_up": ffn1_w_up, "ffn1_w_down": ffn1_w_down,
        "mix_norm": mix_norm, "xattn_norm": xattn_norm, "mem_norm": mem_norm,
        "xattn_w_q": xattn_w_q, "xattn_w_kv": xattn_w_kv, "xattn_w_o": xattn_w_o,
        "ffn2_norm": ffn2_norm, "ffn2_w_gate": ffn2_w_gate, "ffn2_w_up": ffn2_w_up, "ffn2_w_down": ffn2_w_down,
        "ab_w_in": ab_w_in, "ab_conv_w": ab_conv_w, "ab_conv_b": ab_conv_b,
        "ab_conv_ln_g": ab_conv_ln_g, "ab_conv_ln_b": ab_conv_ln_b, "ab_w_out": ab_w_out,
        "mla_w_in": mla_w_in, "mla_q_norm": mla_q_norm, "mla_w_q_b": mla_w_q_b,
        "mla_kv_norm": mla_kv_norm, "mla_w_kv_b": mla_w_kv_b, "mla_w_out": mla_w_out,
        "final_norm": final_norm,
    }
    y_prompt = trunk(x_prompt, mem_prompt, p)
    y_sample = trunk(x_sample, mem_sample, p)
    return (y_prompt, y_sample)
```

```python
import math
import numpy as np
import ml_dtypes
import concourse.bass as bass
import concourse.mybir as mybir
from concourse.bass_utils import run_bass_kernel_spmd

F32 = mybir.dt.float32
BF16 = mybir.dt.bfloat16
U8 = mybir.dt.uint8
AF = mybir.ActivationFunctionType
ALU = mybir.AluOpType

D = 1024
S = 2048
DEPTH = 4
MEM = 256
DFF = 2816
NF = DFF // 128
NT = 4
TT = 512
EPS = 1e-6
NCORES = 8
SEQ_PER_CORE = 3
CONVW = 31
ENGS = ("pe", "act", "dve", "pool", "sp")


class _Ins:
    __slots__ = ("eng", "fn", "deps", "dma", "sig", "sem", "val", "waits", "know", "idx")


class Sched:
    NDS = 8

    def __init__(self):
        self.streams = {e: [] for e in ENGS}
        self.order = []
        self.lastw = {}
        self.readers = {}
        self.regions = {}
        self.rlist = []
        self.uid = 0
        self.cur = {}
        self.region_keys = {}
        self.touched = set()
        self.epoch_marks = []

    def _kill(self, name):
        own = set()
        for k in self.region_keys.pop(name, ()):
            lw = self.lastw.pop(k, None)
            if lw is not None:
                own.add(lw)
            rd = self.readers.pop(k, None)
            if rd:
                own.update(rd[0].values())
                own.update(rd[1])
            self.touched.discard(k)
        if not own:
            return None
        best = {}
        fence = set()
        for ins in own:
            if ins.dma:
                fence.add(ins)
            else:
                cur = best.get(ins.eng)
                if cur is None or cur.idx < ins.idx:
                    best[ins.eng] = ins
        fence.update(best.values())
        return fence

    def region(self, base, off, size):
        self.uid += 1
        name = "%s#%d" % (base, self.uid)
        self.cur[base] = name
        end = off + size
        inherited = set()
        newlist = []
        for ent in self.rlist:
            if not any(s < end and off < e_ for (s, e_) in ent["segs"]):
                newlist.append(ent)
                continue
            if ent["alive"]:
                f = self._kill(ent["name"])
                ent["alive"] = False
                if f is not None:
                    ent["fence"] = f
                self.regions.pop(ent["name"], None)
            inherited.update(ent["fence"])
            segs = []
            for (s, e_) in ent["segs"]:
                if s < off:
                    segs.append((s, min(e_, off)))
                if e_ > end:
                    segs.append((max(s, end), e_))
            segs = [(s, e_) for (s, e_) in segs if e_ > s]
            if segs:
                ent["segs"] = segs
                newlist.append(ent)
        ent = {"name": name, "segs": [(off, end)], "fence": inherited, "alive": True}
        newlist.append(ent)
        self.rlist = newlist
        self.regions[name] = (off, size, inherited)
        self.region_keys[name] = set()

    def new_epoch(self):
        self.epoch_marks.append(len(self.order))

    def op(self, eng, fn, r=(), w=(), dma=False):
        ins = _Ins()
        ins.eng = eng
        ins.fn = fn
        ins.dma = dma
        ins.sig = dma
        ins.idx = len(self.order)
        cur = self.cur
        r = [((cur[k[0]],) + tuple(k[1:])) if k[0] in cur else k for k in r]
        w = [((cur[k[0]],) + tuple(k[1:])) if k[0] in cur else k for k in w]
        deps = set()
        for k in tuple(r) + tuple(w):
            if k not in self.touched:
                self.touched.add(k)
                reg = self.regions.get(k[0])
                if reg is not None:
                    deps.update(reg[2])
                    self.region_keys[k[0]].add(k)
        for k in r:
            lw = self.lastw.get(k)
            if lw is not None:
                deps.add(lw)
        for k in w:
            lw = self.lastw.get(k)
            if lw is not None:
                deps.add(lw)
            rd = self.readers.get(k)
            if rd:
                deps.update(rd[0].values())
                deps.update(rd[1])
        for k in r:
            rd = self.readers.get(k)
            if rd is None:
                rd = self.readers[k] = ({}, [])
            if dma:
                rd[1].append(ins)
            else:
                rd[0][eng] = ins
        for k in w:
            self.lastw[k] = ins
            self.readers[k] = ({}, [])
        deps.discard(ins)
        if eng == "pe":
            deps = {d for d in deps if d.eng != "pe"}
        ins.deps = deps
        self.streams[eng].append(ins)
        self.order.append(ins)
        return ins

    def finalize(self):
        for ins in self.order:
            for d in ins.deps:
                d.sig = True
        marks = set(self.epoch_marks)
        nsem = 0
        engsem = {}
        count = {}

        def fresh():
            nonlocal nsem
            for e in ENGS:
                engsem[e] = nsem
                nsem += 1
                count[e] = 0
        fresh()
        dmasem = {}
        for q in ("sp", "pool"):
            dmasem[q] = list(range(nsem, nsem + self.NDS))
            nsem += self.NDS
        dmacount = {"sp": 0, "pool": 0}
        dmaprev = {"sp": [None] * self.NDS, "pool": [None] * self.NDS}
        know = {e: {} for e in ENGS}
        self.final_waits = {}
        for idx, ins in enumerate(self.order):
            if idx in marks:
                fresh()
            E = ins.eng
            kn = know[E]
            deps = ins.deps
            if ins.dma:
                i = dmacount[E]
                slot = i % self.NDS
                prev = dmaprev[E][slot]
                if prev is not None:
                    deps = set(deps)
                    deps.add(prev)
            waits = []
            dl = sorted(deps, key=lambda d: (d.sem, -d.val))
            for d in dl:
                if kn.get(d.sem, 0) >= d.val:
                    continue
                waits.append((d.sem, d.val))
                for s_, v_ in d.know.items():
                    if kn.get(s_, 0) < v_:
                        kn[s_] = v_
            ins.waits = waits
            if ins.dma:
                ins.sem = dmasem[E][slot]
                ins.val = 16 * (i // self.NDS + 1)
                dmaprev[E][slot] = ins
                dmacount[E] = i + 1
                ins.know = dict(kn)
                ins.know[ins.sem] = ins.val
                self.final_waits[ins.sem] = ins.val
            elif ins.sig:
                count[E] += 1
                ins.sem = engsem[E]
                ins.val = count[E]
                ins.know = dict(kn)
                ins.know[ins.sem] = ins.val
            else:
                ins.sem = -1
                ins.val = 0
                ins.know = None
            ins.deps = None
        self.nsem = nsem
        return nsem

    def emit(self, name, e, sems, final=False):
        for ins in self.streams[name]:
            for (s_, v_) in ins.waits:
                e.wait_ge(sems[s_], v_)
            bi = ins.fn(e)
            if ins.dma:
                bi.then_inc(sems[ins.sem], 16)
            elif ins.sig:
                bi.then_inc(sems[ins.sem], 1)
        if final:
            for s_, v_ in sorted(self.final_waits.items()):
                e.wait_ge(sems[s_], v_)


def _small_layout():
    off = {}
    n = 0

    def add(name, cols):
        nonlocal n
        off[name] = n
        n += cols
    for l in range(DEPTH):
        for nm in ("ffn1_norm", "mix_norm", "xattn_norm", "mem_norm", "ffn2_norm"):
            add((nm, l), 8)
    add(("final_norm", 0), 8)
    for i in range(2):
        add(("mla_q_norm", i), 4)
        add(("mla_kv_norm", i), 2)
        add(("ab_conv_b", i), 4)
        add(("ab_conv_ln_g", i), 4)
        add(("ab_conv_ln_b", i), 4)
        add(("ab_conv_w", i), CONVW * 4)
    return off, n


SMALL_OFF, NSMALL = _small_layout()


def _fm(vec):
    v = np.asarray(vec, dtype=np.float32)
    return np.ascontiguousarray(v.reshape(-1, 128).T)


def _tile_lhsT(W):
    K, M = W.shape
    a = W.reshape(K // 128, 128, M // 128, 128).transpose(2, 1, 0, 3)
    return np.ascontiguousarray(a.reshape(M // 128, 128, (K // 128) * 128))


def _rows_pk(W):
    K, N = W.shape
    a = W.reshape(K // 128, 128, N).transpose(1, 0, 2)
    return np.ascontiguousarray(a.reshape(128, (K // 128) * N))


def prep_weights(inp):
    out = {}
    small = np.zeros((128, NSMALL), np.float32)

    def put(key, arr):
        o = SMALL_OFF[key]
        small[:, o:o + arr.shape[1]] = arr
    for l in range(DEPTH):
        for nm in ("ffn1_norm", "mix_norm", "xattn_norm", "mem_norm", "ffn2_norm"):
            put((nm, l), _fm(inp[nm][l]))
    put(("final_norm", 0), _fm(inp["final_norm"]))
    for i in range(2):
        put(("mla_q_norm", i), _fm(inp["mla_q_norm"][i]))
        put(("mla_kv_norm", i), _fm(inp["mla_kv_norm"][i]))
        put(("ab_conv_b", i), _fm(inp["ab_conv_b"][i]))
        put(("ab_conv_ln_g", i), _fm(inp["ab_conv_ln_g"][i]))
        put(("ab_conv_ln_b", i), _fm(inp["ab_conv_ln_b"][i]))
        cw = np.asarray(inp["ab_conv_w"][i], np.float32)
        a = cw.reshape(CONVW, 4, 128).transpose(2, 0, 1).reshape(128, CONVW * 4)
        put(("ab_conv_w", i), a)
    out["small"] = small

    wgu = np.empty((DEPTH, 2, NF, 128, 2048), np.float32)
    wd = np.empty((DEPTH, 2, DFF, D), np.float32)
    for l in range(DEPTH):
        for wi, pre in enumerate(("ffn1", "ffn2")):
            g = _tile_lhsT(np.asarray(inp[pre + "_w_gate"][l], np.float32))
            u = _tile_lhsT(np.asarray(inp[pre + "_w_up"][l], np.float32))
            wgu[l, wi, :, :, :1024] = g
            wgu[l, wi, :, :, 1024:] = u
            wd[l, wi] = inp[pre + "_w_down"][l]
    out["ffn_wgu"] = wgu
    out["ffn_wd"] = wd

    xa_wq = np.empty((DEPTH, 8, 128, 1024), np.float32)
    xa_wk = np.empty((DEPTH, 8, 128, 1024), np.float32)
    xa_wv = np.empty((DEPTH, 128, 8 * 1024), np.float32)
    xa_wo = np.empty((DEPTH, 8, 128, 1024), np.float32)
    for l in range(DEPTH):
        xa_wq[l] = _tile_lhsT(np.asarray(inp["xattn_w_q"][l], np.float32))
        wkv = np.asarray(inp["xattn_w_kv"][l], np.float32)
        xa_wk[l] = _tile_lhsT(wkv[:, :1024])
        xa_wv[l] = _rows_pk(wkv[:, 1024:])
        xa_wo[l] = _tile_lhsT(np.asarray(inp["xattn_w_o"][l], np.float32))
    out["xa_wq"], out["xa_wk"], out["xa_wv"], out["xa_wo"] = xa_wq, xa_wk, xa_wv, xa_wo

    ab_win = np.empty((2, 12, 128, 1024), np.float32)
    ab_wout = np.empty((2, 8, 128, 1024), np.float32)
    mla_win = np.empty((2, 8, 128, 1024), np.float32)
    mla_wq = np.empty((2, 8, 128, 4 * 384), np.float32)
    mla_wkv = np.empty((2, 8, 128, 2 * 256), np.float32)
    mla_wout = np.empty((2, 8, 128, 1024), np.float32)
    for i in range(2):
        ab_win[i] = _tile_lhsT(np.asarray(inp["ab_w_in"][i], np.float32))
        ab_wout[i] = _tile_lhsT(np.asarray(inp["ab_w_out"][i], np.float32))
        win = np.asarray(inp["mla_w_in"][i], np.float32)
        kr = win[:, 768:832]
        krsw = np.concatenate([kr[:, 32:], kr[:, :32]], axis=1)
        z64 = np.zeros((1024, 64), np.float32)
        win_ext = np.concatenate([win[:, :768], kr, z64, krsw, z64], axis=1)
        mla_win[i] = _tile_lhsT(win_ext)
        wq = np.asarray(inp["mla_w_q_b"][i], np.float32).reshape(512, 8, 192)
        zq = np.zeros((512, 8, 64), np.float32)
        wq_ext = np.concatenate([wq[:, :, :128], wq[:, :, 128:192], zq,
                                 wq[:, :, 160:192], wq[:, :, 128:160], zq], axis=2)
        a = wq_ext.reshape(4, 128, 8, 384).transpose(2, 1, 0, 3)
        mla_wq[i] = a.reshape(8, 128, 4 * 384)
        wkv = np.asarray(inp["mla_w_kv_b"][i], np.float32).reshape(256, 8, 256)
        a = wkv.reshape(2, 128, 8, 256).transpose(2, 1, 0, 3)
        mla_wkv[i] = a.reshape(8, 128, 512)
        mla_wout[i] = _tile_lhsT(np.asarray(inp["mla_w_out"][i], np.float32))
    out["ab_win"], out["ab_wout"] = ab_win, ab_wout
    out["mla_win"], out["mla_wq"], out["mla_wkv"], out["mla_wout"] = mla_win, mla_wq, mla_wkv, mla_wout
    return out


def make_consts():
    c = {}
    n = np.arange(128, dtype=np.float64)
    ang = 2.0 * np.pi * np.outer(n, n) / 128.0
    dftc = np.concatenate([np.cos(ang), -np.sin(ang)], axis=1) / 512.0
    c["dftc"] = dftc.astype(ml_dtypes.bfloat16)
    s = np.arange(S, dtype=np.int64)
    prod = np.outer(s, s) % S
    angs = 2.0 * np.pi * prod.astype(np.float64) / S
    dfts = np.empty((4, 2, 128, 16 * 512), ml_dtypes.bfloat16)
    for cs, fn in enumerate((np.cos, np.sin)):
        m = fn(angs)
        m = m.reshape(16, 128, 4, 512).transpose(2, 1, 0, 3)
        dfts[:, cs] = m.reshape(4, 128, 16 * 512).astype(ml_dtypes.bfloat16)
    c["dfts"] = dfts
    inv_freq = (1.0 / (np.float32(10000.0) ** (np.arange(0, 64, 2, dtype=np.float32) / np.float32(64)))).astype(np.float32)
    angr = (np.arange(S, dtype=np.float32)[:, None] * inv_freq[None, :]).astype(np.float32)
    cos = np.cos(angr).astype(np.float32).T
    sin = np.sin(angr).astype(np.float32).T
    rope = np.zeros((2, 128, S), np.float32)
    rope[0, :32] = cos
    rope[0, 32:64] = cos
    rope[1, :32] = -sin
    rope[1, 32:64] = sin
    c["rope"] = rope
    c["ident"] = np.eye(128, dtype=np.float32).astype(ml_dtypes.bfloat16)
    return c


class Builder:
    def __init__(self, nseq=SEQ_PER_CORE, layers=DEPTH, parts=("ffn1", "mix", "xattn", "ffn2")):
        self.nseq = nseq
        self.layers = layers
        self.parts = parts
        self.nc = bass.Bass("TRN2", target_bir_lowering=False)
        self.S = Sched()
        self._uid = 0

    def dram_in(self, name, shape, dt=F32):
        return self.nc.dram_tensor(name, list(shape), dt, kind="ExternalInput").ap()

    def view(self, off, shape, dt):
        esz = 2 if dt == BF16 else 4
        n = 1
        for s_ in shape[1:]:
            n *= s_
        v = self.arena[0:shape[0], off:off + n * esz].bitcast(dt)
        if len(shape) == 3:
            v = v.rearrange("p (a b) -> p a b", a=shape[1])
        elif len(shape) == 4:
            v = v.rearrange("p (a b c) -> p a b c", a=shape[1], b=shape[2])
        return v

    def build(self):
        nc = self.nc
        S_ = self.S
        nseq = self.nseq
        self.xT = self.dram_in("xT", [nseq, D, S])
        self.memT = self.dram_in("memT", [nseq, D, MEM])
        self.small_d = self.dram_in("small", [128, NSMALL])
        self.ffn_wgu = self.dram_in("ffn_wgu", [DEPTH, 2, NF, 128, 2048])
        self.ffn_wd = self.dram_in("ffn_wd", [DEPTH, 2, DFF, D])
        self.xa_wq = self.dram_in("xa_wq", [DEPTH, 8, 128, 1024])
        self.xa_wk = self.dram_in("xa_wk", [DEPTH, 8, 128, 1024])
        self.xa_wv = self.dram_in("xa_wv", [DEPTH, 128, 8 * 1024])
        self.xa_wo = self.dram_in("xa_wo", [DEPTH, 8, 128, 1024])
        self.ab_win = self.dram_in("ab_win", [2, 12, 128, 1024])
        self.ab_wout = self.dram_in("ab_wout", [2, 8, 128, 1024])
        self.mla_win = self.dram_in("mla_win", [2, 8, 128, 1024])
        self.mla_wq = self.dram_in("mla_wq", [2, 8, 128, 1536])
        self.mla_wkv = self.dram_in("mla_wkv", [2, 8, 128, 512])
        self.mla_wout = self.dram_in("mla_wout", [2, 8, 128, 1024])
        self.dftc_d = self.dram_in("dftc", [128, 256], BF16)
        self.dfts_d = self.dram_in("dfts", [4, 2, 128, 16 * 512], BF16)
        self.rope_d = self.dram_in("rope", [2, 128, S])
        self.ident_d = self.dram_in("ident", [128, 128], BF16)
        self.yT = nc.dram_tensor("yT", [nseq, D, S], F32, kind="ExternalOutput").ap()

        ARENA = 212000
        self.arena = nc.alloc_sbuf_tensor("arena", [128, ARENA], U8)
        self.ps = nc.alloc_psum_tensor("ps", [128, 8, 512], F32)
        o = 0
        self.XT = self.view(o, [128, 8, S], F32); o += 8 * S * 4
        self.HT = self.view(o, [128, 8, S], BF16); o += 8 * S * 2
        self.SMALL = self.view(o, [128, NSMALL], F32); o += NSMALL * 4
        self.ONES = self.view(o, [128, 128], BF16); o += 256
        self.INV = {}
        for n_ in (1024, 512, 256):
            self.INV[n_] = self.view(o, [128, 128], BF16); o += 256
        self.IDENT = self.view(o, [128, 128], BF16); o += 256
        self.DFTC = self.view(o, [128, 256], BF16); o += 512
        self.MEMT = self.view(o, [128, 8, MEM], F32); o += 8 * MEM * 4
        self.SQ = self.view(o, [128, 4, TT], BF16); o += 4 * TT * 2
        self.RS = self.view(o, [128, 2, TT], F32); o += 2 * TT * 4
        self.MRS = self.view(o, [128, MEM], F32); o += MEM * 4
        o = (o + 63) // 64 * 64
        self.A0 = o
        self.AEND = ARENA
        self.sq_i = 0
        self.rs_i = 0
        self.ps_rr = 0

        self.prologue()
        for s_ in range(nseq):
            if s_ > 0:
                S_.new_epoch()
            self.sequence(s_)

        nsem = S_.finalize()
        sems = [nc.alloc_semaphore("s%d" % i) for i in range(nsem)]
        with nc.Block() as block:
            @block.tensor
            def _(e):
                S_.emit("pe", e, sems)

            @block.scalar
            def _(e):
                S_.emit("act", e, sems)

            @block.vector
            def _(e):
                S_.emit("dve", e, sems)

            @block.gpsimd
            def _(e):
                S_.emit("pool", e, sems)

            @block.sync
            def _(e):
                S_.emit("sp", e, sems, final=True)
        return nc

    def op(self, *a, **k):
        return self.S.op(*a, **k)

    def bank(self):
        b = self.ps_rr
        self.ps_rr = (b + 1) % 8
        return b

    def g(self, key, c):
        o = SMALL_OFF[key] + c
        return self.SMALL[:, o:o + 1]

    def dma_w(self, dst, src, wkeys):
        self.op("pool", lambda e: e.dma_start(out=dst, in_=src), r=(), w=wkeys, dma=True)

    def dma_sp(self, dst, src, r=(), w=()):
        self.op("sp", lambda e: e.dma_start(out=dst, in_=src), r=r, w=w, dma=True)

    def mm_group(self, out, pairs, r, w):
        n = len(pairs)

        def fn(e):
            bi = None
            for i, (l_, r_) in enumerate(pairs):
                bi = e.matmul(out, l_, r_, start=(i == 0), stop=(i == n - 1))
            return bi
        return self.op("pe", fn, r=r, w=w)

    def prologue(self):
        self.dma_sp(self.SMALL, self.small_d, w=[("SMALL",)])
        self.dma_sp(self.IDENT, self.ident_d, w=[("IDENT",)])
        self.dma_sp(self.DFTC, self.dftc_d, w=[("DFTC",)])
        self.op("dve", lambda e: e.memset(self.ONES, 1.0), w=[("ONES",)])
        for n_ in (1024, 512, 256):
            self.op("dve", (lambda n_: (lambda e: e.memset(self.INV[n_], 1.0 / n_)))(n_), w=[("INV", n_)])

    def sequence(self, si):
        for c in range(8):
            self.dma_sp(self.XT[:, c, :], self.xT[si, c * 128:(c + 1) * 128, :],
                        w=[("XT", c, t) for t in range(NT)])
        self.dma_sp(self.MEMT, self.memT[si].rearrange("(c p) m -> p c m", p=128), w=[("MEMT",)])
        self.norm_stats([(self.MEMT[:, c, :], ("MEMT",)) for c in range(8)], 1024, width=MEM,
                        out=self.MRS, outkey=("MRS",))
        for l in range(self.layers):
            if "ffn1" in self.parts:
                self.ffn(l, 0)
            if "mix" in self.parts:
                if l % 2 == 0:
                    self.mixer_ab(l // 2, l)
                else:
                    self.mixer_mla(l // 2, l)
            if "xattn" in self.parts:
                self.xattn(l)
            if "ffn2" in self.parts:
                self.ffn(l, 1)
        self.final(si)

    def norm_stats(self, chunks, n, width=TT, out=None, outkey=None):
        b = self.bank()
        pst = self.ps[:, b, 0:width]
        nch = len(chunks)
        for c, (src, key) in enumerate(chunks):
            i = self.sq_i
            self.sq_i = (i + 1) % 4
            sq = self.SQ[:, i, 0:width]
            self.act(sq, src, AF.Square, r=[key], w=[("SQ", i)])
            self.op("pe", self._mm1(pst, self.INV[n], sq, c == 0, c == nch - 1),
                    r=[("SQ", i), ("INV", n)], w=[("ps", b)])
        j = self.rs_i
        self.rs_i = (j + 1) % 2
        rs = self.RS[:, j, 0:width]
        self.act(rs, pst, AF.Ln, r=[("ps", b)], w=[("RS", j)], bias=EPS, scale=1.0)
        if out is None:
            self.act(pst, rs, AF.Exp, r=[("RS", j)], w=[("ps", b)], scale=-0.5)
        else:
            self.act(out, rs, AF.Exp, r=[("RS", j)], w=[outkey], scale=-0.5)
        return b

    def recip_act(self, out, in_, r, w, width=TT):
        j = self.rs_i
        self.rs_i = (j + 1) % 2
        rs = self.RS[:, j, 0:width]
        self.act(rs, in_, AF.Ln, r=r, w=[("RS", j)])
        self.act(out, rs, AF.Exp, r=[("RS", j)], w=w, scale=-1.0)

    def rmsnorm_x(self, gkey):
        for t in range(NT):
            ts = slice(t * TT, (t + 1) * TT)
            b = self.norm_stats([(self.XT[:, c, ts], ("XT", c, t)) for c in range(8)], 1024)
            for c in range(8):
                self.stt("dve", self.HT[:, c, ts], self.XT[:, c, ts], self.g(gkey, c), self.ps[:, b, :],
                         ALU.mult, ALU.mult, r=[("XT", c, t), ("ps", b), ("SMALL",)], w=[("HT", c, t)])

    def stt(self, eng, out, in0, scalar, in1, op0, op1, r, w):
        return self.op(eng, lambda e: e.scalar_tensor_tensor(out=out, in0=in0, scalar=scalar, in1=in1, op0=op0, op1=op1), r=r, w=w)

    def tt(self, eng, out, in0, in1, op, r, w):
        return self.op(eng, lambda e: e.tensor_tensor(out=out, in0=in0, in1=in1, op=op), r=r, w=w)

    def ts_(self, eng, out, in0, s1, s2, op0, op1, r, w):
        return self.op(eng, lambda e: e.tensor_scalar(out=out, in0=in0, scalar1=s1, scalar2=s2, op0=op0, op1=op1), r=r, w=w)

    def act(self, out, in_, func, r, w, bias=None, scale=None):
        kw = {}
        if bias is not None:
            kw["bias"] = bias
        if scale is not None:
            kw["scale"] = scale
        return self.op("act", lambda e: e.activation(out=out, in_=in_, func=func, **kw), r=r, w=w)

    def copy(self, eng, out, in_, r, w):
        if eng == "act":
            return self.op("act", lambda e: e.activation(out=out, in_=in_, func=AF.Copy), r=r, w=w)
        return self.op(eng, lambda e: e.tensor_copy(out=out, in_=in_), r=r, w=w)

    def recip(self, out, in_, r, w):
        return self.op("dve", lambda e: e.reciprocal(out=out, in_=in_), r=r, w=w)

    def memset(self, eng, ap, val, w):
        return self.op(eng, lambda e: e.memset(ap, val), w=w)

    def ffn(self, l, which):
        S_ = self.S
        gkey = ("ffn1_norm" if which == 0 else "ffn2_norm", l)
        self.rmsnorm_x(gkey)
        groups = [[0, 1, 2, 3, 4, 5], [6, 7, 8, 9, 10, 11], [12, 13, 14, 15, 16], [17, 18, 19, 20, 21]]
        GMAX = 6
        NWGU = 4
        NWD = 12
        o = self.A0
        S_.region("ACTH", o, GMAX * S * 2)
        ACTH = self.view(o, [128, GMAX, S], BF16); o += GMAX * S * 2
        S_.region("WGU", o, NWGU * 4096)
        WGU = self.view(o, [128, NWGU, 2, 8, 128], BF16) if False else self.view(o, [128, NWGU, 2048], BF16); o += NWGU * 4096
        S_.region("WD", o, NWD * 2048)
        WD = self.view(o, [128, NWD, 1024], BF16); o += NWD * 2048
        S_.region("SG", o, 2 * TT * 4)
        SG = self.view(o, [128, 2, TT], F32); o += 2 * TT * 4
        assert o <= self.AEND
        wgu_i = 0
        wd_i = 0
        sg_i = 0
        for grp in groups:
            dslots = {}
            for fl, f in enumerate(grp):
                ws = wgu_i % NWGU
                wgu_i += 1
                self.dma_w(WGU[:, ws, :], self.ffn_wgu[l, which, f], [("WGU", ws)])
                ds = wd_i % NWD
                wd_i += 1
                dslots[f] = ds
                self.dma_w(WD[:, ds, :], self.ffn_wd[l, which, f * 128:(f + 1) * 128, :], [("WD", ds)])
                for t in range(NT):
                    ts = slice(t * TT, (t + 1) * TT)
                    bg = self.bank()
                    bu = self.bank()
                    hk = [("HT", k, t) for k in range(8)]
                    self.mm_group(self.ps[:, bg, :],
                                  [(WGU[:, ws, k * 128:(k + 1) * 128], self.HT[:, k, ts]) for k in range(8)],
                                  r=hk + [("WGU", ws)], w=[("ps", bg)])
                    self.mm_group(self.ps[:, bu, :],
                                  [(WGU[:, ws, 1024 + k * 128:1024 + (k + 1) * 128], self.HT[:, k, ts]) for k in range(8)],
                                  r=hk + [("WGU", ws)], w=[("ps", bu)])
                    sg = sg_i % 2
                    sg_i += 1
                    self.op("act", (lambda sg, bg: (lambda e: e.activation(out=SG[:, sg, :], in_=self.ps[:, bg, :], func=AF.Silu)))(sg, bg),
                            r=[("ps", bg)], w=[("SG", sg)])
                    self.op("dve", (lambda sg, bu, fl, ts: (lambda e: e.tensor_tensor(
                        out=ACTH[:, fl, ts], in0=SG[:, sg, :], in1=self.ps[:, bu, :], op=ALU.mult)))(sg, bu, fl, ts),
                        r=[("SG", sg), ("ps", bu)], w=[("ACTH", fl, t)])
            ng = len(grp)
            for t in range(NT):
                ts = slice(t * TT, (t + 1) * TT)
                for d in range(8):
                    bd = self.bank()
                    self.mm_group(self.ps[:, bd, :],
                                  [(WD[:, dslots[f], d * 128:(d + 1) * 128], ACTH[:, fl, ts]) for fl, f in enumerate(grp)],
                                  r=[("ACTH", fl, t) for fl in range(ng)] + [("WD", dslots[f]) for f in grp],
                                  w=[("ps", bd)])
                    self.op("dve", (lambda d, ts, bd: (lambda e: e.scalar_tensor_tensor(
                        out=self.XT[:, d, ts], in0=self.ps[:, bd, :], scalar=0.5, in1=self.XT[:, d, ts],
                        op0=ALU.mult, op1=ALU.add)))(d, ts, bd),
                        r=[("ps", bd), ("XT", d, t)], w=[("XT", d, t)])

    def final(self, si):
        S_ = self.S
        o = self.A0
        S_.region("YO", o, 2 * S * 4)
        YO = self.view(o, [128, 2, S], F32)
        gkey = ("final_norm", 0)
        banks = []
        for t in range(NT):
            ts = slice(t * TT, (t + 1) * TT)
            banks.append(self.norm_stats([(self.XT[:, c, ts], ("XT", c, t)) for c in range(8)], 1024))
        for c in range(8):
            yo = c % 2
            for t in range(NT):
                ts = slice(t * TT, (t + 1) * TT)
                b = banks[t]
                self.op("dve", (lambda c, ts, b, yo: (lambda e: e.scalar_tensor_tensor(
                    out=YO[:, yo, ts], in0=self.XT[:, c, ts], scalar=self.g(gkey, c), in1=self.ps[:, b, :],
                    op0=ALU.mult, op1=ALU.mult)))(c, ts, b, yo),
                    r=[("XT", c, t), ("ps", b), ("SMALL",)], w=[("YO", yo)])
            self.dma_sp(self.yT[si, c * 128:(c + 1) * 128, :], YO[:, yo, :], r=[("YO", yo)], w=[("OUT", si, c)])

    def wring(self, name, off, nslots):
        self.S.region(name, off, nslots * 2048)
        self._wr = (name, self.view(off, [128, nslots, 1024], BF16), nslots)
        self._wr_i = 0
        return off + nslots * 2048

    def wload(self, src):
        name, W, n = self._wr
        ws = self._wr_i % n
        self._wr_i += 1
        self.dma_w(W[:, ws, :], src, [(name, ws)])
        return W[:, ws, :], (name, ws)

    def evac_copy(self, out, in_, r, w, scale=None):
        self._ev = getattr(self, "_ev", 0) + 1
        if scale is not None:
            return self.op("act", lambda e: e.activation(out=out, in_=in_, func=AF.Copy, scale=scale), r=r, w=w)
        if self._ev % 2 == 0:
            return self.copy("act", out, in_, r, w)
        return self.copy("dve", out, in_, r, w)

    def xattn(self, l):
        S_ = self.S
        self.rmsnorm_x(("xattn_norm", l))
        o = self.A0
        S_.region("QT", o, 8 * S * 2); QT = self.view(o, [128, 8, S], BF16); o += 8 * S * 2
        S_.region("MNT", o, 8 * MEM * 2); MNT = self.view(o, [128, 8, MEM], BF16); o += 8 * MEM * 2
        S_.region("KT", o, 8 * MEM * 2); KT = self.view(o, [128, 8, MEM], BF16); o += 8 * MEM * 2
        S_.region("V", o, 2 * 1024 * 2); V = self.view(o, [128, 2, 1024], BF16); o += 2 * 1024 * 2
        oWV = o
        S_.region("WV", o, 8 * 1024 * 2); WV = self.view(o, [128, 8, 1024], BF16); o += 8 * 1024 * 2
        o = self.wring("W", o, 4)
        S_.region("PT", o, 4 * TT * 2); PT = self.view(o, [128, 4, TT], BF16); o += 4 * TT * 2
        S_.region("RD", o, 2 * TT * 4); RD = self.view(o, [128, 2, TT], F32); o += 2 * TT * 4
        assert o <= self.AEND
        for c in range(8):
            self.stt("dve", MNT[:, c, :], self.MEMT[:, c, :], self.g(("mem_norm", l), c), self.MRS,
                     ALU.mult, ALU.mult, r=[("MEMT",), ("MRS",), ("SMALL",)], w=[("MNT", c)])
        mk = [("MNT", c) for c in range(8)]
        self.dma_w(WV.rearrange("p k n -> p (k n)"), self.xa_wv[l], [("WV",)])
        for oc in range(8):
            W, wk = self.wload(self.xa_wq[l, oc])
            for t in range(NT):
                ts = slice(t * TT, (t + 1) * TT)
                b = self.bank()
                self.mm_group(self.ps[:, b, :], [(W[:, k * 128:(k + 1) * 128], self.HT[:, k, ts]) for k in range(8)],
                              r=[("HT", k, t) for k in range(8)] + [wk], w=[("ps", b)])
                self.evac_copy(QT[:, oc, ts], self.ps[:, b, :], r=[("ps", b)], w=[("QT", oc, t)], scale=1.0 / 16.0)
        for oc in range(8):
            W, wk = self.wload(self.xa_wk[l, oc])
            b = self.bank()
            self.mm_group(self.ps[:, b, 0:MEM], [(W[:, k * 128:(k + 1) * 128], MNT[:, k, :]) for k in range(8)],
                          r=mk + [wk], w=[("ps", b)])
            self.evac_copy(KT[:, oc, :], self.ps[:, b, 0:MEM], r=[("ps", b)], w=[("KT", oc)])
        for mt in range(2):
            for nh in range(2):
                b = self.bank()
                self.mm_group(self.ps[:, b, :],
                              [(MNT[:, k, mt * 128:(mt + 1) * 128], WV[:, k, nh * 512:(nh + 1) * 512]) for k in range(8)],
                              r=mk + [("WV",)], w=[("ps", b)])
                self.evac_copy(V[:, mt, nh * 512:(nh + 1) * 512], self.ps[:, b, :], r=[("ps", b)], w=[("V", mt, nh)])
        W8 = self.out_proj_load(oWV, lambda oc: self.xa_wo[l, oc])
        pt_i = 0
        rd_i = 0
        items = [(h, t) for h in range(4) for t in range(NT)]
        sbanks = [0, 1, 2, 3]
        sb_i = 0
        obanks = [4, 5, 6, 7]
        ob_i = 0
        pend = {}

        def scores(h, t):
            nonlocal sb_i, pt_i
            ts = slice(t * TT, (t + 1) * TT)
            slots = []
            for mt in range(2):
                bs = sbanks[sb_i % 4]; sb_i += 1
                self.mm_group(self.ps[:, bs, :],
                              [(KT[:, 2 * h + dc, mt * 128:(mt + 1) * 128], QT[:, 2 * h + dc, ts]) for dc in range(2)],
                              r=[("KT", 2 * h), ("KT", 2 * h + 1), ("QT", 2 * h, t), ("QT", 2 * h + 1, t)], w=[("ps", bs)])
                sl = pt_i % 4
                pt_i += 1
                self.act(PT[:, sl, :], self.ps[:, bs, :], AF.Exp, r=[("ps", bs)], w=[("PT", sl)])
                slots.append(sl)
            pend[(h, t)] = slots

        def pv(h, t):
            nonlocal ob_i, rd_i
            ts = slice(t * TT, (t + 1) * TT)
            slots = pend.pop((h, t))
            bd = obanks[ob_i % 4]; ob_i += 1
            self.mm_group(self.ps[:, bd, :], [(self.ONES, PT[:, sl, :]) for sl in slots],
                          r=[("PT", sl) for sl in slots] + [("ONES",)], w=[("ps", bd)])
            rj = rd_i % 2
            rd_i += 1
            self.recip_act(RD[:, rj, :], self.ps[:, bd, :], r=[("ps", bd)], w=[("RD", rj)])
            for dc in range(2):
                c = 2 * h + dc
                bo = obanks[ob_i % 4]; ob_i += 1
                self.mm_group(self.ps[:, bo, :],
                              [(V[:, mt, c * 128:(c + 1) * 128], PT[:, slots[mt], :]) for mt in range(2)],
                              r=[("PT", sl) for sl in slots] + [("V", mt, c // 4) for mt in range(2)], w=[("ps", bo)])
                self.tt("dve", self.HT[:, c, ts], self.ps[:, bo, :], RD[:, rj, :], ALU.mult,
                        r=[("ps", bo), ("RD", rj)], w=[("HT", c, t)])
        for n_, it in enumerate(items):
            scores(*it)
            if n_ >= 1:
                pv(*items[n_ - 1])
        pv(*items[-1])
        self.out_proj(W8)

    def out_proj_load(self, off, wsrc):
        self.S.region("W8", off, 8 * 2048)
        W8 = self.view(off, [128, 8, 1024], BF16)
        for oc in range(8):
            self.dma_w(W8[:, oc, :], wsrc(oc), [("W8", oc)])
        return W8

    def out_proj(self, W8):
        for t in range(NT):
            ts = slice(t * TT, (t + 1) * TT)
            for oc in range(8):
                b = self.bank()
                self.mm_group(self.ps[:, b, :], [(W8[:, oc, k * 128:(k + 1) * 128], self.HT[:, k, ts]) for k in range(8)],
                              r=[("HT", k, t) for k in range(8)] + [("W8", oc)], w=[("ps", b)])
                self.tt("dve", self.XT[:, oc, ts], self.ps[:, b, :], self.XT[:, oc, ts], ALU.add,
                        r=[("ps", b), ("XT", oc, t)], w=[("XT", oc, t)])

    def mixer_ab(self, i, l):
        S_ = self.S
        self.rmsnorm_x(("mix_norm", l))
        A0 = self.A0
        PADL = 16
        HPW = S + 32
        o = A0
        S_.region("UF", o, 4 * S * 2); UF = self.view(o, [128, 4, S], BF16); o += 4 * S * 2
        S_.region("HP", o, 4 * HPW * 2); HP = self.view(o, [128, 4, HPW], BF16); o += 4 * HPW * 2
        oB = o
        o = self.wring("W", o, 4)
        S_.region("SGT", o, 2 * TT * 4); SGT = self.view(o, [128, 2, TT], F32); o += 2 * TT * 4
        assert o <= self.AEND
        for ch in range(4):
            self.memset("dve", HP[:, ch, 0:PADL], 0.0, w=[("HP", ch, "padl")])
            self.memset("dve", HP[:, ch, PADL + S:HPW], 0.0, w=[("HP", ch, "padr")])
        for oc in range(4):
            W, wk = self.wload(self.ab_win[i, oc])
            for t in range(NT):
                ts = slice(t * TT, (t + 1) * TT)
                b = self.bank()
                self.mm_group(self.ps[:, b, :], [(W[:, k * 128:(k + 1) * 128], self.HT[:, k, ts]) for k in range(8)],
                              r=[("HT", k, t) for k in range(8)] + [wk], w=[("ps", b)])
                self.evac_copy(UF[:, oc, ts], self.ps[:, b, :], r=[("ps", b)], w=[("UF", oc, t)])
        sg_i = 0
        for ch in range(4):
            Wa, wka = self.wload(self.ab_win[i, 4 + ch])
            Wg, wkg = self.wload(self.ab_win[i, 8 + ch])
            for t in range(NT):
                ts = slice(t * TT, (t + 1) * TT)
                ba = self.bank()
                bg = self.bank()
                hk = [("HT", k, t) for k in range(8)]
                self.mm_group(self.ps[:, ba, :], [(Wa[:, k * 128:(k + 1) * 128], self.HT[:, k, ts]) for k in range(8)],
                              r=hk + [wka], w=[("ps", ba)])
                self.mm_group(self.ps[:, bg, :], [(Wg[:, k * 128:(k + 1) * 128], self.HT[:, k, ts]) for k in range(8)],
                              r=hk + [wkg], w=[("ps", bg)])
                sg = sg_i % 2
                sg_i += 1
                self.act(SGT[:, sg, :], self.ps[:, bg, :], AF.Sigmoid, r=[("ps", bg)], w=[("SGT", sg)])
                self.tt("dve", HP[:, ch, PADL + t * TT:PADL + (t + 1) * TT], SGT[:, sg, :], self.ps[:, ba, :], ALU.mult,
                        r=[("SGT", sg), ("ps", ba)], w=[("HP", ch, t)])
        o = oB
        S_.region("AB", o, 16 * 4 * 256 * 2); AB = self.view(o, [128, 16, 4, 256], BF16); o += 16 * 4 * 256 * 2
        NCS = 2
        S_.region("CS", o, NCS * 8 * 512 * 2); CS = self.view(o, [128, NCS, 8, 512], BF16); o += NCS * 8 * 512 * 2
        oDG0 = o
        S_.region("DG0", o, CONVW * 128 * 2); DG0 = self.view(o, [128, CONVW, 128], BF16); o += CONVW * 128 * 2
        assert o <= self.AEND

        def build_diag(DGv, key, ch):
            for j in range(CONVW):
                self.ts_("dve", DGv[:, j, :], self.IDENT, self.g(("ab_conv_w", i), j * 4 + ch), None, ALU.mult, ALU.bypass,
                         r=[("IDENT",), ("SMALL",)], w=[key])
        build_diag(DG0, ("DG0",), 0)
        for tt_ in range(16):
            for gp in range(2):
                b = self.bank()
                for gg in range(2):
                    g_ = 2 * gp + gg
                    self.mm_group(self.ps[:, b, gg * 256:(gg + 1) * 256],
                                  [(UF[:, g_, tt_ * 128:(tt_ + 1) * 128], self.DFTC)],
                                  r=[("UF", g_, tt_ // 4), ("DFTC",)], w=[("ps", b)])
                self.evac_copy(AB[:, tt_, 2 * gp:2 * gp + 2, :].rearrange("p a b -> p (a b)"), self.ps[:, b, :],
                               r=[("ps", b)], w=[("AB", tt_, gp)])
        W8 = self.out_proj_load(A0, lambda oc: self.ab_wout[i, oc])
        cs_i = 0
        for st in range(4):
            banks = [self.bank() for _ in range(4)]
            npiece = 4
            for pc in range(npiece):
                cs = pc // 2
                half = pc % 2
                sl = cs_i % NCS
                cs_i += 1
                self.dma_sp(CS[:, sl, :, :].rearrange("p a b -> p (a b)"),
                            self.dfts_d[st, cs, :, half * 8 * 512:(half + 1) * 8 * 512], w=[("CS", sl)])
                for g_ in range(4):
                    b = banks[g_]
                    pairs = [(AB[:, half * 8 + s8, g_, cs * 128:(cs + 1) * 128], CS[:, sl, s8, :]) for s8 in range(8)]
                    self.mm_acc(self.ps[:, b, :], pairs, first=(pc == 0), last=(pc == npiece - 1),
                                r=[("AB", half * 8 + s8, g_ // 2) for s8 in range(8)] + [("CS", sl)], w=[("ps", b)])
            for g_ in range(4):
                self.evac_copy(self.HT[:, g_, st * TT:(st + 1) * TT], self.ps[:, banks[g_], :],
                               r=[("ps", banks[g_])], w=[("HT", g_, st)])
        o = oB
        S_.region("DG1", o, CONVW * 128 * 2); DG1 = self.view(o, [128, CONVW, 128], BF16); o += CONVW * 128 * 2
        S_.region("CO", o, 4 * S * 4); CO = self.view(o, [128, 4, S], F32); o += 4 * S * 4
        assert o <= oDG0
        for ch in range(4):
            DGv, dkey = (DG0, ("DG0",)) if ch % 2 == 0 else (DG1, ("DG1",))
            if ch > 0:
                build_diag(DGv, dkey, ch)
            for t in range(NT):
                ts = slice(t * TT, (t + 1) * TT)
                b = self.bank()
                self.mm_group(self.ps[:, b, :],
                              [(DGv[:, j, :], HP[:, ch, t * TT + j + 1:t * TT + j + 1 + TT]) for j in range(CONVW)],
                              r=[dkey, ("HP", ch, "padl"), ("HP", ch, "padr")] + [("HP", ch, tq) for tq in range(NT)],
                              w=[("ps", b)])
                self.act(CO[:, ch, ts], self.ps[:, b, :], AF.Identity, r=[("ps", b), ("SMALL",)], w=[("CO", ch, t)],
                         bias=self.g(("ab_conv_b", i), ch))
        self.ps_rr = 0
        lnb = []
        for t in range(NT):
            ts = slice(t * TT, (t + 1) * TT)
            bm = self.bank()
            bq = self.bank()
            lnb.append((bm, bq))
            for ch in range(4):
                i1 = self.sq_i; self.sq_i = (i1 + 1) % 4
                self.act(self.SQ[:, i1, :], CO[:, ch, ts], AF.Square, r=[("CO", ch, t)], w=[("SQ", i1)])
                self.op("pe", self._mm1(self.ps[:, bq, :], self.INV[512], self.SQ[:, i1, :], ch == 0, ch == 3),
                        r=[("SQ", i1), ("INV", 512)], w=[("ps", bq)])
                i2 = self.sq_i; self.sq_i = (i2 + 1) % 4
                self.copy("dve", self.SQ[:, i2, :], CO[:, ch, ts], r=[("CO", ch, t)], w=[("SQ", i2)])
                self.op("pe", self._mm1(self.ps[:, bm, :], self.INV[512], self.SQ[:, i2, :], ch == 0, ch == 3),
                        r=[("SQ", i2), ("INV", 512)], w=[("ps", bm)])
        for t in range(NT):
            ts = slice(t * TT, (t + 1) * TT)
            bm, bq = lnb[t]
            j = self.rs_i; self.rs_i = (j + 1) % 2
            rs = self.RS[:, j, :]
            self.act(rs, self.ps[:, bm, :], AF.Square, r=[("ps", bm)], w=[("RS", j)])
            self.tt("dve", rs, self.ps[:, bq, :], rs, ALU.subtract, r=[("ps", bq), ("RS", j)], w=[("RS", j)])
            self.act(rs, rs, AF.Ln, r=[("RS", j)], w=[("RS", j)], bias=EPS, scale=1.0)
            self.act(self.ps[:, bq, :], rs, AF.Exp, r=[("RS", j)], w=[("ps", bq)], scale=-0.5)
            for ch in range(4):
                self.tt("dve", CO[:, ch, ts], CO[:, ch, ts], self.ps[:, bm, :], ALU.subtract,
                        r=[("CO", ch, t), ("ps", bm)], w=[("CO", ch, t)])
                self.tt("dve", CO[:, ch, ts], CO[:, ch, ts], self.ps[:, bq, :], ALU.mult,
                        r=[("CO", ch, t), ("ps", bq)], w=[("CO", ch, t)])
                self.act(self.HT[:, 4 + ch, ts], CO[:, ch, ts], AF.Silu, r=[("CO", ch, t), ("SMALL",)], w=[("HT", 4 + ch, t)],
                         bias=self.g(("ab_conv_ln_b", i), ch), scale=self.g(("ab_conv_ln_g", i), ch))
        self.out_proj(W8)

    def _mm1(self, out, lhsT, rhs, start, stop):
        return lambda e: e.matmul(out, lhsT, rhs, start=start, stop=stop)

    def mm_acc(self, out, pairs, first, last, r, w):
        n = len(pairs)

        def fn(e):
            bi = None
            for i_, (l_, r_) in enumerate(pairs):
                bi = e.matmul(out, l_, r_, start=(first and i_ == 0), stop=(last and i_ == n - 1))
            return bi
        return self.op("pe", fn, r=r, w=w)

    def mixer_mla(self, i, l):
        S_ = self.S
        self.rmsnorm_x(("mix_norm", l))
        o = self.A0
        oCQN = o
        S_.region("CQN", o, 4 * S * 2); CQN = self.view(o, [128, 4, S], BF16); o += 4 * S * 2
        S_.region("CKVN", o, 2 * S * 2); CKVN = self.view(o, [128, 2, S], BF16); o += 2 * S * 2
        S_.region("KR", o, S * 2); KR = self.view(o, [128, S], BF16); o += S * 2
        S_.region("ROPE", o, 2 * S * 4); ROPE = self.view(o, [128, 2, S], F32); o += 2 * S * 4
        S_.region("T12", o, 2 * TT * 4); T12 = self.view(o, [128, 2, TT], F32); o += 2 * TT * 4
        oP = o
        S_.region("WIN", o, 8 * 2048); WIN = self.view(o, [128, 8, 1024], BF16); o += 8 * 2048
        S_.region("CQ32", o, 4 * TT * 4); CQ32 = self.view(o, [128, 4, TT], F32); o += 4 * TT * 4
        S_.region("CKV32", o, 2 * TT * 4); CKV32 = self.view(o, [128, 2, TT], F32); o += 2 * TT * 4
        assert o <= self.AEND
        self.dma_sp(ROPE[:, 0, :], self.rope_d[0], w=[("ROPE", 0)])
        self.dma_sp(ROPE[:, 1, :], self.rope_d[1], w=[("ROPE", 1)])
        for oc in range(8):
            self.dma_w(WIN[:, oc, :], self.mla_win[i, oc], [("WIN", oc)])

        def rope_combine(dst, ba, bb, ts, wkey):
            self.tt("dve", T12[:, 0, :], self.ps[:, ba, :], ROPE[:, 0, ts], ALU.mult,
                    r=[("ps", ba), ("ROPE", 0)], w=[("T12", 0)])
            self.tt("dve", T12[:, 1, :], self.ps[:, bb, :], ROPE[:, 1, ts], ALU.mult,
                    r=[("ps", bb), ("ROPE", 1)], w=[("T12", 1)])
            self.tt("dve", dst, T12[:, 0, :], T12[:, 1, :], ALU.add,
                    r=[("T12", 0), ("T12", 1)], w=[wkey])

        for t in range(NT):
            ts = slice(t * TT, (t + 1) * TT)
            hk = [("HT", k, t) for k in range(8)]

            def proj(oc):
                b = self.bank()
                self.mm_group(self.ps[:, b, :], [(WIN[:, oc, k * 128:(k + 1) * 128], self.HT[:, k, ts]) for k in range(8)],
                              r=hk + [("WIN", oc)], w=[("ps", b)])
                return b
            for oc in range(4):
                b = proj(oc)
                self.copy("act", CQ32[:, oc, :], self.ps[:, b, :], r=[("ps", b)], w=[("CQ32", oc)])
            b = self.norm_stats([(CQ32[:, oc, :], ("CQ32", oc)) for oc in range(4)], 512)
            for oc in range(4):
                self.stt("dve", CQN[:, oc, ts], CQ32[:, oc, :], self.g(("mla_q_norm", i), oc), self.ps[:, b, :],
                         ALU.mult, ALU.mult, r=[("CQ32", oc), ("ps", b), ("SMALL",)], w=[("CQN", oc, t)])
            for oc in range(2):
                b = proj(4 + oc)
                self.copy("act", CKV32[:, oc, :], self.ps[:, b, :], r=[("ps", b)], w=[("CKV32", oc)])
            b = self.norm_stats([(CKV32[:, oc, :], ("CKV32", oc)) for oc in range(2)], 256)
            for oc in range(2):
                self.stt("dve", CKVN[:, oc, ts], CKV32[:, oc, :], self.g(("mla_kv_norm", i), oc), self.ps[:, b, :],
                         ALU.mult, ALU.mult, r=[("CKV32", oc), ("ps", b), ("SMALL",)], w=[("CKVN", oc, t)])
            ba = proj(6)
            bb = proj(7)
            rope_combine(KR[:, ts], ba, bb, ts, ("KR", t))

        o = oP
        S_.region("QN", o, S * 2); QN = self.view(o, [128, S], BF16); o += S * 2
        S_.region("QR", o, S * 2); QR = self.view(o, [128, S], BF16); o += S * 2
        S_.region("KN", o, S * 2); KN = self.view(o, [128, S], BF16); o += S * 2
        S_.region("VH", o, S * 2); VH = self.view(o, [128, 16, 128], BF16); o += S * 2
        S_.region("WQ", o, 2 * 3072); WQ = self.view(o, [128, 2, 1536], BF16); o += 2 * 3072
        S_.region("WKV", o, 2 * 1024); WKV = self.view(o, [128, 2, 512], BF16); o += 2 * 1024
        NPT = 6
        S_.region("PT", o, NPT * TT * 2); PT = self.view(o, [128, NPT, TT], BF16); o += NPT * TT * 2
        S_.region("RD", o, 2 * TT * 4); RD = self.view(o, [128, 2, TT], F32); o += 2 * TT * 4
        oW = o
        assert o <= self.AEND
        SCALE = 1.0 / math.sqrt(192.0)
        pj = [7, 0, 1]
        pj_i = 0
        sb = [0, 1, 2]
        sb_i = 0
        DO = [(3, 4), (5, 6)]
        do_i = 0
        pt_i = 0
        rd_i = 0
        LA = 2
        for h in range(8):
            s = h % 2
            self.dma_w(WQ[:, s, :], self.mla_wq[i, h], [("WQ", s)])
            self.dma_w(WKV[:, s, :], self.mla_wkv[i, h], [("WKV", s)])
            for t in range(NT):
                ts = slice(t * TT, (t + 1) * TT)
                cq = [("CQN", k, t) for k in range(4)]
                b = pj[pj_i % 3]; pj_i += 1
                self.mm_group(self.ps[:, b, :], [(WQ[:, s, k * 384:k * 384 + 128], CQN[:, k, ts]) for k in range(4)],
                              r=cq + [("WQ", s)], w=[("ps", b)])
                self.copy("act", QN[:, ts], self.ps[:, b, :], r=[("ps", b)], w=[("QN", t)])
                ba = pj[pj_i % 3]; pj_i += 1
                self.mm_group(self.ps[:, ba, :], [(WQ[:, s, k * 384 + 128:k * 384 + 256], CQN[:, k, ts]) for k in range(4)],
                              r=cq + [("WQ", s)], w=[("ps", ba)])
                bb = pj[pj_i % 3]; pj_i += 1
                self.mm_group(self.ps[:, bb, :], [(WQ[:, s, k * 384 + 256:k * 384 + 384], CQN[:, k, ts]) for k in range(4)],
                              r=cq + [("WQ", s)], w=[("ps", bb)])
                rope_combine(QR[:, ts], ba, bb, ts, ("QR", t))
                b = pj[pj_i % 3]; pj_i += 1
                self.mm_group(self.ps[:, b, :], [(WKV[:, s, k * 256:k * 256 + 128], CKVN[:, k, ts]) for k in range(2)],
                              r=[("CKVN", k, t) for k in range(2)] + [("WKV", s)], w=[("ps", b)])
                self.copy("act", KN[:, ts], self.ps[:, b, :], r=[("ps", b)], w=[("KN", t)])
            for kt4 in range(4):
                b = pj[pj_i % 3]; pj_i += 1
                for q in range(4):
                    kt = kt4 * 4 + q
                    self.mm_group(self.ps[:, b, q * 128:(q + 1) * 128],
                                  [(CKVN[:, k, kt * 128:(kt + 1) * 128], WKV[:, s, k * 256 + 128:k * 256 + 256]) for k in range(2)],
                                  r=[("CKVN", k, kt4) for k in range(2)] + [("WKV", s)], w=[("ps", b)])
                self.copy("dve", VH[:, kt4 * 4:(kt4 + 1) * 4, :].rearrange("p a b -> p (a b)"), self.ps[:, b, :],
                          r=[("ps", b)], w=[("VH", kt4)])
            if h == 7:
                W8 = self.out_proj_load(oCQN, lambda oc: self.mla_wout[i, oc])
            items = [(t, kt) for t in range(NT) for kt in range(16)]
            slots = {}
            for n_ in range(len(items) + LA):
                if n_ < len(items):
                    t, kt = items[n_]
                    ts = slice(t * TT, (t + 1) * TT)
                    bs = sb[sb_i % 3]; sb_i += 1
                    self.mm_group(self.ps[:, bs, :],
                                  [(KN[:, kt * 128:(kt + 1) * 128], QN[:, ts]), (KR[:, kt * 128:(kt + 1) * 128], QR[:, ts])],
                                  r=[("KN", kt // 4), ("QN", t), ("KR", kt // 4), ("QR", t)], w=[("ps", bs)])
                    sl = pt_i % NPT; pt_i += 1
                    slots[(t, kt)] = sl
                    self.act(PT[:, sl, :], self.ps[:, bs, :], AF.Exp, r=[("ps", bs)], w=[("PT", sl)], scale=SCALE)
                if n_ >= LA:
                    t, kt = items[n_ - LA]
                    ts = slice(t * TT, (t + 1) * TT)
                    sl = slots.pop((t, kt))
                    BD, BO = DO[do_i % 2]
                    self.mm_acc(self.ps[:, BD, :], [(self.ONES, PT[:, sl, :])], first=(kt == 0), last=(kt == 15),
                                r=[("PT", sl), ("ONES",)], w=[("ps", BD)])
                    self.mm_acc(self.ps[:, BO, :], [(VH[:, kt, :], PT[:, sl, :])], first=(kt == 0), last=(kt == 15),
                                r=[("PT", sl), ("VH", kt // 4)], w=[("ps", BO)])
                    if kt == 15:
                        do_i += 1
                        rj = rd_i % 2; rd_i += 1
                        self.recip_act(RD[:, rj, :], self.ps[:, BD, :], r=[("ps", BD)], w=[("RD", rj)])
                        self.tt("dve", self.HT[:, h, ts], self.ps[:, BO, :], RD[:, rj, :], ALU.mult,
                                r=[("ps", BO), ("RD", rj)], w=[("HT", h, t)])
        self.ps_rr = 0
        self.out_proj(W8)


def _run(inputs, nseq=SEQ_PER_CORE, layers=DEPTH, parts=("ffn1", "mix", "xattn", "ffn2"), ncores=NCORES, trace=False):
    xs = np.concatenate([np.asarray(inputs["x_prompt"], np.float32), np.asarray(inputs["x_sample"], np.float32)], axis=0)
    ms = np.concatenate([np.asarray(inputs["mem_prompt"], np.float32), np.asarray(inputs["mem_sample"], np.float32)], axis=0)
    w = prep_weights(inputs)
    w.update(make_consts())
    b = Builder(nseq=nseq, layers=layers, parts=parts)
    nc = b.build()
    in_maps = []
    for c in range(ncores):
        sl = slice(c * nseq, (c + 1) * nseq)
        m = dict(w)
        m["xT"] = np.ascontiguousarray(xs[sl].transpose(0, 2, 1))
        m["memT"] = np.ascontiguousarray(ms[sl].transpose(0, 2, 1))
        in_maps.append(m)
    res = run_bass_kernel_spmd(nc, in_maps, core_ids=list(range(ncores)), trace=trace)
    ys = np.concatenate([np.asarray(r["yT"]).transpose(0, 2, 1) for r in res.results], axis=0)
    return np.ascontiguousarray(ys.astype(np.float32)), res


def kernel(**inputs):
    ys, _ = _run(inputs)
    nb = inputs["x_prompt"].shape[0]
    return (ys[:nb], ys[nb:])
```

```python
import math
import numpy as np
import ml_dtypes
import concourse.bass as bass
import concourse.mybir as mybir
from concourse.bass_utils import run_bass_kernel_spmd

F32 = mybir.dt.float32
BF16 = mybir.dt.bfloat16
U8 = mybir.dt.uint8
AF = mybir.ActivationFunctionType
ALU = mybir.AluOpType

D = 1024
S = 2048
DEPTH = 4
MEM = 256
DFF = 2816
NF = DFF // 128
NT = 4
TT = 512
EPS = 1e-6
NCORES = 8
SEQ_PER_CORE = 3
CONVW = 31
ENGS = ("pe", "act", "dve", "pool", "sp")


class _Ins:
    __slots__ = ("eng", "fn", "deps", "dma", "sig", "sem", "val", "waits", "know", "idx")


class Sched:
    NDS = 8

    def __init__(self):
        self.streams = {e: [] for e in ENGS}
        self.order = []
        self.lastw = {}
        self.readers = {}
        self.regions = {}
        self.rlist = []
        self.uid = 0
        self.cur = {}
        self.region_keys = {}
        self.touched = set()
        self.epoch_marks = []

    def _kill(self, name):
        own = set()
        for k in self.region_keys.pop(name, ()):
            lw = self.lastw.pop(k, None)
            if lw is not None:
                own.add(lw)
            rd = self.readers.pop(k, None)
            if rd:
                own.update(rd[0].values())
                own.update(rd[1])
            self.touched.discard(k)
        if not own:
            return None
        best = {}
        fence = set()
        for ins in own:
            if ins.dma:
                fence.add(ins)
            else:
                cur = best.get(ins.eng)
                if cur is None or cur.idx < ins.idx:
                    best[ins.eng] = ins
        fence.update(best.values())
        return fence

    def region(self, base, off, size):
        self.uid += 1
        name = "%s#%d" % (base, self.uid)
        self.cur[base] = name
        end = off + size
        inherited = set()
        newlist = []
        for ent in self.rlist:
            if not any(s < end and off < e_ for (s, e_) in ent["segs"]):
                newlist.append(ent)
                continue
            if ent["alive"]:
                f = self._kill(ent["name"])
                ent["alive"] = False
                if f is not None:
                    ent["fence"] = f
                self.regions.pop(ent["name"], None)
            inherited.update(ent["fence"])
            segs = []
            for (s, e_) in ent["segs"]:
                if s < off:
                    segs.append((s, min(e_, off)))
                if e_ > end:
                    segs.append((max(s, end), e_))
            segs = [(s, e_) for (s, e_) in segs if e_ > s]
            if segs:
                ent["segs"] = segs
                newlist.append(ent)
        ent = {"name": name, "segs": [(off, end)], "fence": inherited, "alive": True}
        newlist.append(ent)
        self.rlist = newlist
        self.regions[name] = (off, size, inherited)
        self.region_keys[name] = set()

    def new_epoch(self):
        self.epoch_marks.append(len(self.order))

    def op(self, eng, fn, r=(), w=(), dma=False):
        ins = _Ins()
        ins.eng = eng
        ins.fn = fn
        ins.dma = dma
        ins.sig = dma
        ins.idx = len(self.order)
        cur = self.cur
        r = [((cur[k[0]],) + tuple(k[1:])) if k[0] in cur else k for k in r]
        w = [((cur[k[0]],) + tuple(k[1:])) if k[0] in cur else k for k in w]
        deps = set()
        for k in tuple(r) + tuple(w):
            if k not in self.touched:
                self.touched.add(k)
                reg = self.regions.get(k[0])
                if reg is not None:
                    deps.update(reg[2])
                    self.region_keys[k[0]].add(k)
        for k in r:
            lw = self.lastw.get(k)
            if lw is not None:
                deps.add(lw)
        for k in w:
            lw = self.lastw.get(k)
            if lw is not None:
                deps.add(lw)
            rd = self.readers.get(k)
            if rd:
                deps.update(rd[0].values())
                deps.update(rd[1])
        for k in r:
            rd = self.readers.get(k)
            if rd is None:
                rd = self.readers[k] = ({}, [])
            if dma:
                rd[1].append(ins)
            else:
                rd[0][eng] = ins
        for k in w:
            self.lastw[k] = ins
            self.readers[k] = ({}, [])
        deps.discard(ins)
        if eng == "pe":
            deps = {d for d in deps if d.eng != "pe"}
        ins.deps = deps
        self.streams[eng].append(ins)
        self.order.append(ins)
        return ins

    def finalize(self):
        for ins in self.order:
            for d in ins.deps:
                d.sig = True
        marks = set(self.epoch_marks)
        nsem = 0
        engsem = {}
        count = {}

        def fresh():
            nonlocal nsem
            for e in ENGS:
                engsem[e] = nsem
                nsem += 1
                count[e] = 0
        fresh()
        dmasem = {}
        for q in ("sp", "pool"):
            dmasem[q] = list(range(nsem, nsem + self.NDS))
            nsem += self.NDS
        dmacount = {"sp": 0, "pool": 0}
        dmaprev = {"sp": [None] * self.NDS, "pool": [None] * self.NDS}
        know = {e: {} for e in ENGS}
        self.final_waits = {}
        for idx, ins in enumerate(self.order):
            if idx in marks:
                fresh()
            E = ins.eng
            kn = know[E]
            deps = ins.deps
            if ins.dma:
                i = dmacount[E]
                slot = i % self.NDS
                prev = dmaprev[E][slot]
                if prev is not None:
                    deps = set(deps)
                    deps.add(prev)
            waits = []
            dl = sorted(deps, key=lambda d: (d.sem, -d.val))
            for d in dl:
                if kn.get(d.sem, 0) >= d.val:
                    continue
                waits.append((d.sem, d.val))
                for s_, v_ in d.know.items():
                    if kn.get(s_, 0) < v_:
                        kn[s_] = v_
            ins.waits = waits
            if ins.dma:
                ins.sem = dmasem[E][slot]
                ins.val = 16 * (i // self.NDS + 1)
                dmaprev[E][slot] = ins
                dmacount[E] = i + 1
                ins.know = dict(kn)
                ins.know[ins.sem] = ins.val
                self.final_waits[ins.sem] = ins.val
            elif ins.sig:
                count[E] += 1
                ins.sem = engsem[E]
                ins.val = count[E]
                ins.know = dict(kn)
                ins.know[ins.sem] = ins.val
            else:
                ins.sem = -1
                ins.val = 0
                ins.know = None
            ins.deps = None
        self.nsem = nsem
        return nsem

    def emit(self, name, e, sems, final=False):
        for ins in self.streams[name]:
            for (s_, v_) in ins.waits:
                e.wait_ge(sems[s_], v_)
            bi = ins.fn(e)
            if ins.dma:
                bi.then_inc(sems[ins.sem], 16)
            elif ins.sig:
                bi.then_inc(sems[ins.sem], 1)
        if final:
            for s_, v_ in sorted(self.final_waits.items()):
                e.wait_ge(sems[s_], v_)


def _small_layout():
    off = {}
    n = 0

    def add(name, cols):
        nonlocal n
        off[name] = n
        n += cols
    for l in range(DEPTH):
        for nm in ("ffn1_norm", "mix_norm", "xattn_norm", "mem_norm", "ffn2_norm"):
            add((nm, l), 8)
    add(("final_norm", 0), 8)
    for i in range(2):
        add(("mla_q_norm", i), 4)
        add(("mla_kv_norm", i), 2)
        add(("ab_conv_b", i), 4)
        add(("ab_conv_ln_g", i), 4)
        add(("ab_conv_ln_b", i), 4)
        add(("ab_conv_w", i), CONVW * 4)
    return off, n


SMALL_OFF, NSMALL = _small_layout()


def _fm(vec):
    v = np.asarray(vec, dtype=np.float32)
    return np.ascontiguousarray(v.reshape(-1, 128).T)


def _tile_lhsT(W):
    K, M = W.shape
    a = W.reshape(K // 128, 128, M // 128, 128).transpose(2, 1, 0, 3)
    return np.ascontiguousarray(a.reshape(M // 128, 128, (K // 128) * 128))


def _rows_pk(W):
    K, N = W.shape
    a = W.reshape(K // 128, 128, N).transpose(1, 0, 2)
    return np.ascontiguousarray(a.reshape(128, (K // 128) * N))


def prep_weights(inp):
    out = {}
    small = np.zeros((128, NSMALL), np.float32)

    def put(key, arr):
        o = SMALL_OFF[key]
        small[:, o:o + arr.shape[1]] = arr
    for l in range(DEPTH):
        for nm in ("ffn1_norm", "mix_norm", "xattn_norm", "mem_norm", "ffn2_norm"):
            put((nm, l), _fm(inp[nm][l]))
    put(("final_norm", 0), _fm(inp["final_norm"]))
    for i in range(2):
        put(("mla_q_norm", i), _fm(inp["mla_q_norm"][i]))
        put(("mla_kv_norm", i), _fm(inp["mla_kv_norm"][i]))
        put(("ab_conv_b", i), _fm(inp["ab_conv_b"][i]))
        put(("ab_conv_ln_g", i), _fm(inp["ab_conv_ln_g"][i]))
        put(("ab_conv_ln_b", i), _fm(inp["ab_conv_ln_b"][i]))
        cw = np.asarray(inp["ab_conv_w"][i], np.float32)
        a = cw.reshape(CONVW, 4, 128).transpose(2, 0, 1).reshape(128, CONVW * 4)
        put(("ab_conv_w", i), a)
    out["small"] = small

    wgu = np.empty((DEPTH, 2, NF, 128, 2048), np.float32)
    wd = np.empty((DEPTH, 2, DFF, D), np.float32)
    for l in range(DEPTH):
        for wi, pre in enumerate(("ffn1", "ffn2")):
            g = _tile_lhsT(np.asarray(inp[pre + "_w_gate"][l], np.float32))
            u = _tile_lhsT(np.asarray(inp[pre + "_w_up"][l], np.float32))
            wgu[l, wi, :, :, :1024] = g
            wgu[l, wi, :, :, 1024:] = u
            wd[l, wi] = inp[pre + "_w_down"][l]
    out["ffn_wgu"] = wgu
    out["ffn_wd"] = wd

    xa_wq = np.empty((DEPTH, 8, 128, 1024), np.float32)
    xa_wk = np.empty((DEPTH, 8, 128, 1024), np.float32)
    xa_wv = np.empty((DEPTH, 128, 8 * 1024), np.float32)
    xa_wo = np.empty((DEPTH, 8, 128, 1024), np.float32)
    for l in range(DEPTH):
        xa_wq[l] = _tile_lhsT(np.asarray(inp["xattn_w_q"][l], np.float32))
        wkv = np.asarray(inp["xattn_w_kv"][l], np.float32)
        xa_wk[l] = _tile_lhsT(wkv[:, :1024])
        xa_wv[l] = _rows_pk(wkv[:, 1024:])
        xa_wo[l] = _tile_lhsT(np.asarray(inp["xattn_w_o"][l], np.float32))
    out["xa_wq"], out["xa_wk"], out["xa_wv"], out["xa_wo"] = xa_wq, xa_wk, xa_wv, xa_wo

    ab_win = np.empty((2, 12, 128, 1024), np.float32)
    ab_wout = np.empty((2, 8, 128, 1024), np.float32)
    mla_win = np.empty((2, 8, 128, 1024), np.float32)
    mla_wq = np.empty((2, 8, 128, 4 * 384), np.float32)
    mla_wkv = np.empty((2, 8, 128, 2 * 256), np.float32)
    mla_wout = np.empty((2, 8, 128, 1024), np.float32)
    for i in range(2):
        ab_win[i] = _tile_lhsT(np.asarray(inp["ab_w_in"][i], np.float32))
        ab_wout[i] = _tile_lhsT(np.asarray(inp["ab_w_out"][i], np.float32))
        win = np.asarray(inp["mla_w_in"][i], np.float32)
        kr = win[:, 768:832]
        krsw = np.concatenate([kr[:, 32:], kr[:, :32]], axis=1)
        z64 = np.zeros((1024, 64), np.float32)
        win_ext = np.concatenate([win[:, :768], kr, z64, krsw, z64], axis=1)
        mla_win[i] = _tile_lhsT(win_ext)
        wq = np.asarray(inp["mla_w_q_b"][i], np.float32).reshape(512, 8, 192)
        zq = np.zeros((512, 8, 64), np.float32)
        wq_ext = np.concatenate([wq[:, :, :128], wq[:, :, 128:192], zq,
                                 wq[:, :, 160:192], wq[:, :, 128:160], zq], axis=2)
        a = wq_ext.reshape(4, 128, 8, 384).transpose(2, 1, 0, 3)
        mla_wq[i] = a.reshape(8, 128, 4 * 384)
        wkv = np.asarray(inp["mla_w_kv_b"][i], np.float32).reshape(256, 8, 256)
        a = wkv.reshape(2, 128, 8, 256).transpose(2, 1, 0, 3)
        mla_wkv[i] = a.reshape(8, 128, 512)
        mla_wout[i] = _tile_lhsT(np.asarray(inp["mla_w_out"][i], np.float32))
    out["ab_win"], out["ab_wout"] = ab_win, ab_wout
    out["mla_win"], out["mla_wq"], out["mla_wkv"], out["mla_wout"] = mla_win, mla_wq, mla_wkv, mla_wout
    return out


def make_consts():
    c = {}
    n = np.arange(128, dtype=np.float64)
    ang = 2.0 * np.pi * np.outer(n, n) / 128.0
    dftc = np.concatenate([np.cos(ang), -np.sin(ang)], axis=1) / 512.0
    c["dftc"] = dftc.astype(ml_dtypes.bfloat16)
    s = np.arange(S, dtype=np.int64)
    prod = np.outer(s, s) % S
    angs = 2.0 * np.pi * prod.astype(np.float64) / S
    dfts = np.empty((4, 2, 128, 16 * 512), ml_dtypes.bfloat16)
    for cs, fn in enumerate((np.cos, np.sin)):
        m = fn(angs)
        m = m.reshape(16, 128, 4, 512).transpose(2, 1, 0, 3)
        dfts[:, cs] = m.reshape(4, 128, 16 * 512).astype(ml_dtypes.bfloat16)
    c["dfts"] = dfts
    inv_freq = (1.0 / (np.float32(10000.0) ** (np.arange(0, 64, 2, dtype=np.float32) / np.float32(64)))).astype(np.float32)
    angr = (np.arange(S, dtype=np.float32)[:, None] * inv_freq[None, :]).astype(np.float32)
    cos = np.cos(angr).astype(np.float32).T
    sin = np.sin(angr).astype(np.float32).T
    rope = np.zeros((2, 128, S), np.float32)
    rope[0, :32] = cos
    rope[0, 32:64] = cos
    rope[1, :32] = -sin
    rope[1, 32:64] = sin
    c["rope"] = rope
    c["ident"] = np.eye(128, dtype=np.float32).astype(ml_dtypes.bfloat16)
    return c


class Builder:
    def __init__(self, nseq=SEQ_PER_CORE, layers=DEPTH, parts=("ffn1", "mix", "xattn", "ffn2")):
        self.nseq = nseq
        self.layers = layers
        self.parts = parts
        self.nc = bass.Bass("TRN2", target_bir_lowering=False)
        self.S = Sched()
        self._uid = 0

    def dram_in(self, name, shape, dt=F32):
        return self.nc.dram_tensor(name, list(shape), dt, kind="ExternalInput").ap()

    def view(self, off, shape, dt):
        esz = 2 if dt == BF16 else 4
        n = 1
        for s_ in shape[1:]:
            n *= s_
        v = self.arena[0:shape[0], off:off + n * esz].bitcast(dt)
        if len(shape) == 3:
            v = v.rearrange("p (a b) -> p a b", a=shape[1])
        elif len(shape) == 4:
            v = v.rearrange("p (a b c) -> p a b c", a=shape[1], b=shape[2])
        return v

    def build(self):
        nc = self.nc
        S_ = self.S
        nseq = self.nseq
        self.xT = self.dram_in("xT", [nseq, D, S])
        self.memT = self.dram_in("memT", [nseq, D, MEM])
        self.small_d = self.dram_in("small", [128, NSMALL])
        self.ffn_wgu = self.dram_in("ffn_wgu", [DEPTH, 2, NF, 128, 2048])
        self.ffn_wd = self.dram_in("ffn_wd", [DEPTH, 2, DFF, D])
        self.xa_wq = self.dram_in("xa_wq", [DEPTH, 8, 128, 1024])
        self.xa_wk = self.dram_in("xa_wk", [DEPTH, 8, 128, 1024])
        self.xa_wv = self.dram_in("xa_wv", [DEPTH, 128, 8 * 1024])
        self.xa_wo = self.dram_in("xa_wo", [DEPTH, 8, 128, 1024])
        self.ab_win = self.dram_in("ab_win", [2, 12, 128, 1024])
        self.ab_wout = self.dram_in("ab_wout", [2, 8, 128, 1024])
        self.mla_win = self.dram_in("mla_win", [2, 8, 128, 1024])
        self.mla_wq = self.dram_in("mla_wq", [2, 8, 128, 1536])
        self.mla_wkv = self.dram_in("mla_wkv", [2, 8, 128, 512])
        self.mla_wout = self.dram_in("mla_wout", [2, 8, 128, 1024])
        self.dftc_d = self.dram_in("dftc", [128, 256], BF16)
        self.dfts_d = self.dram_in("dfts", [4, 2, 128, 16 * 512], BF16)
        self.rope_d = self.dram_in("rope", [2, 128, S])
        self.ident_d = self.dram_in("ident", [128, 128], BF16)
        self.yT = nc.dram_tensor("yT", [nseq, D, S], F32, kind="ExternalOutput").ap()

        ARENA = 212000
        self.arena = nc.alloc_sbuf_tensor("arena", [128, ARENA], U8)
        self.ps = nc.alloc_psum_tensor("ps", [128, 8, 512], F32)
        o = 0
        self.XT = self.view(o, [128, 8, S], F32); o += 8 * S * 4
        self.HT = self.view(o, [128, 8, S], BF16); o += 8 * S * 2
        self.SMALL = self.view(o, [128, NSMALL], F32); o += NSMALL * 4
        self.ONES = self.view(o, [128, 128], BF16); o += 256
        self.INV = {}
        for n_ in (1024, 512, 256):
            self.INV[n_] = self.view(o, [128, 128], BF16); o += 256
        self.IDENT = self.view(o, [128, 128], BF16); o += 256
        self.DFTC = self.view(o, [128, 256], BF16); o += 512
        self.MEMT = self.view(o, [128, 8, MEM], F32); o += 8 * MEM * 4
        self.SQ = self.view(o, [128, 4, TT], BF16); o += 4 * TT * 2
        self.RS = self.view(o, [128, 2, TT], F32); o += 2 * TT * 4
        self.MRS = self.view(o, [128, MEM], F32); o += MEM * 4
        o = (o + 63) // 64 * 64
        self.A0 = o
        self.AEND = ARENA
        self.sq_i = 0
        self.rs_i = 0
        self.ps_rr = 0

        self.prologue()
        for s_ in range(nseq):
            if s_ > 0:
                S_.new_epoch()
            self.sequence(s_)

        nsem = S_.finalize()
        sems = [nc.alloc_semaphore("s%d" % i) for i in range(nsem)]
        with nc.Block() as block:
            @block.tensor
            def _(e):
                S_.emit("pe", e, sems)

            @block.scalar
            def _(e):
                S_.emit("act", e, sems)

            @block.vector
            def _(e):
                S_.emit("dve", e, sems)

            @block.gpsimd
            def _(e):
                S_.emit("pool", e, sems)

            @block.sync
            def _(e):
                S_.emit("sp", e, sems, final=True)
        return nc

    def op(self, *a, **k):
        return self.S.op(*a, **k)

    def bank(self):
        b = self.ps_rr
        self.ps_rr = (b + 1) % 8
        return b

    def g(self, key, c):
        o = SMALL_OFF[key] + c
        return self.SMALL[:, o:o + 1]

    def dma_w(self, dst, src, wkeys):
        self.op("pool", lambda e: e.dma_start(out=dst, in_=src), r=(), w=wkeys, dma=True)

    def dma_sp(self, dst, src, r=(), w=()):
        self.op("sp", lambda e: e.dma_start(out=dst, in_=src), r=r, w=w, dma=True)

    def mm_group(self, out, pairs, r, w):
        n = len(pairs)

        def fn(e):
            bi = None
            for i, (l_, r_) in enumerate(pairs):
                bi = e.matmul(out, l_, r_, start=(i == 0), stop=(i == n - 1))
            return bi
        return self.op("pe", fn, r=r, w=w)

    def prologue(self):
        self.dma_sp(self.SMALL, self.small_d, w=[("SMALL",)])
        self.dma_sp(self.IDENT, self.ident_d, w=[("IDENT",)])
        self.dma_sp(self.DFTC, self.dftc_d, w=[("DFTC",)])
        self.op("dve", lambda e: e.memset(self.ONES, 1.0), w=[("ONES",)])
        for n_ in (1024, 512, 256):
            self.op("dve", (lambda n_: (lambda e: e.memset(self.INV[n_], 1.0 / n_)))(n_), w=[("INV", n_)])

    def sequence(self, si):
        for c in range(8):
            self.dma_sp(self.XT[:, c, :], self.xT[si, c * 128:(c + 1) * 128, :],
                        w=[("XT", c, t) for t in range(NT)])
        self.dma_sp(self.MEMT, self.memT[si].rearrange("(c p) m -> p c m", p=128), w=[("MEMT",)])
        self.norm_stats([(self.MEMT[:, c, :], ("MEMT",)) for c in range(8)], 1024, width=MEM,
                        out=self.MRS, outkey=("MRS",))
        phases = []
        for l in range(self.layers):
            if "ffn1" in self.parts:
                phases.append((("ffn1_norm", l), lambda nxt, l=l: self.ffn(l, 0, nxt)))
            if "mix" in self.parts:
                if l % 2 == 0:
                    phases.append((("mix_norm", l), lambda nxt, l=l: self.mixer_ab(l // 2, l, nxt)))
                else:
                    phases.append((("mix_norm", l), lambda nxt, l=l: self.mixer_mla(l // 2, l, nxt)))
            if "xattn" in self.parts:
                phases.append((("xattn_norm", l), lambda nxt, l=l: self.xattn(l, nxt)))
            if "ffn2" in self.parts:
                phases.append((("ffn2_norm", l), lambda nxt, l=l: self.ffn(l, 1, nxt)))
        self.pre_norm = None
        for n_, (gk, fn) in enumerate(phases):
            nxt = phases[n_ + 1][0] if n_ + 1 < len(phases) else None
            fn(nxt)
        self.final(si)

    def norm_stats(self, chunks, n, width=TT, out=None, outkey=None):
        b = self.bank()
        pst = self.ps[:, b, 0:width]
        nch = len(chunks)
        for c, (src, key) in enumerate(chunks):
            i = self.sq_i
            self.sq_i = (i + 1) % 4
            sq = self.SQ[:, i, 0:width]
            self.act(sq, src, AF.Square, r=[key], w=[("SQ", i)])
            self.op("pe", self._mm1(pst, self.INV[n], sq, c == 0, c == nch - 1),
                    r=[("SQ", i), ("INV", n)], w=[("ps", b)])
        j = self.rs_i
        self.rs_i = (j + 1) % 2
        rs = self.RS[:, j, 0:width]
        self.act(rs, pst, AF.Ln, r=[("ps", b)], w=[("RS", j)], bias=EPS, scale=1.0)
        if out is None:
            self.act(pst, rs, AF.Exp, r=[("RS", j)], w=[("ps", b)], scale=-0.5)
        else:
            self.act(out, rs, AF.Exp, r=[("RS", j)], w=[outkey], scale=-0.5)
        return b

    def recip_act(self, out, in_, r, w, width=TT):
        j = self.rs_i
        self.rs_i = (j + 1) % 2
        rs = self.RS[:, j, 0:width]
        self.act(rs, in_, AF.Ln, r=r, w=[("RS", j)])
        self.act(out, rs, AF.Exp, r=[("RS", j)], w=w, scale=-1.0)

    def norm_tile(self, gkey, t):
        ts = slice(t * TT, (t + 1) * TT)
        b = self.norm_stats([(self.XT[:, c, ts], ("XT", c, t)) for c in range(8)], 1024)
        for c in range(8):
            self.stt("dve", self.HT[:, c, ts], self.XT[:, c, ts], self.g(gkey, c), self.ps[:, b, :],
                     ALU.mult, ALU.mult, r=[("XT", c, t), ("ps", b), ("SMALL",)], w=[("HT", c, t)])

    def rmsnorm_x(self, gkey):
        if self.pre_norm == gkey:
            self.pre_norm = None
            return
        for t in range(NT):
            self.norm_tile(gkey, t)

    def stt(self, eng, out, in0, scalar, in1, op0, op1, r, w):
        return self.op(eng, lambda e: e.scalar_tensor_tensor(out=out, in0=in0, scalar=scalar, in1=in1, op0=op0, op1=op1), r=r, w=w)

    def tt(self, eng, out, in0, in1, op, r, w):
        return self.op(eng, lambda e: e.tensor_tensor(out=out, in0=in0, in1=in1, op=op), r=r, w=w)

    def ts_(self, eng, out, in0, s1, s2, op0, op1, r, w):
        return self.op(eng, lambda e: e.tensor_scalar(out=out, in0=in0, scalar1=s1, scalar2=s2, op0=op0, op1=op1), r=r, w=w)

    def act(self, out, in_, func, r, w, bias=None, scale=None):
        kw = {}
        if bias is not None:
            kw["bias"] = bias
        if scale is not None:
            kw["scale"] = scale
        return self.op("act", lambda e: e.activation(out=out, in_=in_, func=func, **kw), r=r, w=w)

    def copy(self, eng, out, in_, r, w):
        if eng == "act":
            return self.op("act", lambda e: e.activation(out=out, in_=in_, func=AF.Copy), r=r, w=w)
        return self.op(eng, lambda e: e.tensor_copy(out=out, in_=in_), r=r, w=w)

    def recip(self, out, in_, r, w):
        return self.op("dve", lambda e: e.reciprocal(out=out, in_=in_), r=r, w=w)

    def memset(self, eng, ap, val, w):
        return self.op(eng, lambda e: e.memset(ap, val), w=w)

    def ffn(self, l, which, nxt=None):
        S_ = self.S
        gkey = ("ffn1_norm" if which == 0 else "ffn2_norm", l)
        self.rmsnorm_x(gkey)
        groups = [[0, 1, 2, 3, 4, 5], [6, 7, 8, 9, 10, 11], [12, 13, 14, 15, 16], [17, 18, 19, 20, 21]]
        GMAX = 6
        NWGU = 4
        NWD = 12
        o = self.A0
        S_.region("ACTH", o, GMAX * S * 2)
        ACTH = self.view(o, [128, GMAX, S], BF16); o += GMAX * S * 2
        S_.region("WGU", o, NWGU * 4096)
        WGU = self.view(o, [128, NWGU, 2, 8, 128], BF16) if False else self.view(o, [128, NWGU, 2048], BF16); o += NWGU * 4096
        S_.region("WD", o, NWD * 2048)
        WD = self.view(o, [128, NWD, 1024], BF16); o += NWD * 2048
        S_.region("SG", o, 2 * TT * 4)
        SG = self.view(o, [128, 2, TT], F32); o += 2 * TT * 4
        assert o <= self.AEND
        wgu_i = 0
        wd_i = 0
        sg_i = 0
        for grp in groups:
            dslots = {}
            for fl, f in enumerate(grp):
                ws = wgu_i % NWGU
                wgu_i += 1
                self.dma_w(WGU[:, ws, :], self.ffn_wgu[l, which, f], [("WGU", ws)])
                ds = wd_i % NWD
                wd_i += 1
                dslots[f] = ds
                self.dma_w(WD[:, ds, :], self.ffn_wd[l, which, f * 128:(f + 1) * 128, :], [("WD", ds)])
                for t in range(NT):
                    ts = slice(t * TT, (t + 1) * TT)
                    bg = self.bank()
                    bu = self.bank()
                    hk = [("HT", k, t) for k in range(8)]
                    self.mm_group(self.ps[:, bg, :],
                                  [(WGU[:, ws, k * 128:(k + 1) * 128], self.HT[:, k, ts]) for k in range(8)],
                                  r=hk + [("WGU", ws)], w=[("ps", bg)])
                    self.mm_group(self.ps[:, bu, :],
                                  [(WGU[:, ws, 1024 + k * 128:1024 + (k + 1) * 128], self.HT[:, k, ts]) for k in range(8)],
                                  r=hk + [("WGU", ws)], w=[("ps", bu)])
                    sg = sg_i % 2
                    sg_i += 1
                    self.op("act", (lambda sg, bg: (lambda e: e.activation(out=SG[:, sg, :], in_=self.ps[:, bg, :], func=AF.Silu)))(sg, bg),
                            r=[("ps", bg)], w=[("SG", sg)])
                    self.op("dve", (lambda sg, bu, fl, ts: (lambda e: e.tensor_tensor(
                        out=ACTH[:, fl, ts], in0=SG[:, sg, :], in1=self.ps[:, bu, :], op=ALU.mult)))(sg, bu, fl, ts),
                        r=[("SG", sg), ("ps", bu)], w=[("ACTH", fl, t)])
            ng = len(grp)
            lastg = grp is groups[-1]
            for t in range(NT):
                ts = slice(t * TT, (t + 1) * TT)
                if lastg and nxt is not None and t >= 2:
                    self.norm_tile(nxt, t - 2)
                for d in range(8):
                    bd = self.bank()
                    self.mm_group(self.ps[:, bd, :],
                                  [(WD[:, dslots[f], d * 128:(d + 1) * 128], ACTH[:, fl, ts]) for fl, f in enumerate(grp)],
                                  r=[("ACTH", fl, t) for fl in range(ng)] + [("WD", dslots[f]) for f in grp],
                                  w=[("ps", bd)])
                    self.op("dve", (lambda d, ts, bd: (lambda e: e.scalar_tensor_tensor(
                        out=self.XT[:, d, ts], in0=self.ps[:, bd, :], scalar=0.5, in1=self.XT[:, d, ts],
                        op0=ALU.mult, op1=ALU.add)))(d, ts, bd),
                        r=[("ps", bd), ("XT", d, t)], w=[("XT", d, t)])
        if nxt is not None:
            self.norm_tile(nxt, NT - 2)
            self.norm_tile(nxt, NT - 1)
            self.pre_norm = nxt

    def final(self, si):
        S_ = self.S
        o = self.A0
        S_.region("YO", o, 2 * S * 4)
        YO = self.view(o, [128, 2, S], F32)
        gkey = ("final_norm", 0)
        banks = []
        for t in range(NT):
            ts = slice(t * TT, (t + 1) * TT)
            banks.append(self.norm_stats([(self.XT[:, c, ts], ("XT", c, t)) for c in range(8)], 1024))
        for c in range(8):
            yo = c % 2
            for t in range(NT):
                ts = slice(t * TT, (t + 1) * TT)
                b = banks[t]
                self.op("dve", (lambda c, ts, b, yo: (lambda e: e.scalar_tensor_tensor(
                    out=YO[:, yo, ts], in0=self.XT[:, c, ts], scalar=self.g(gkey, c), in1=self.ps[:, b, :],
                    op0=ALU.mult, op1=ALU.mult)))(c, ts, b, yo),
                    r=[("XT", c, t), ("ps", b), ("SMALL",)], w=[("YO", yo)])
            self.dma_sp(self.yT[si, c * 128:(c + 1) * 128, :], YO[:, yo, :], r=[("YO", yo)], w=[("OUT", si, c)])

    def wring(self, name, off, nslots):
        self.S.region(name, off, nslots * 2048)
        self._wr = (name, self.view(off, [128, nslots, 1024], BF16), nslots)
        self._wr_i = 0
        return off + nslots * 2048

    def wload(self, src):
        name, W, n = self._wr
        ws = self._wr_i % n
        self._wr_i += 1
        self.dma_w(W[:, ws, :], src, [(name, ws)])
        return W[:, ws, :], (name, ws)

    def evac_copy(self, out, in_, r, w, scale=None):
        self._ev = getattr(self, "_ev", 0) + 1
        if scale is not None:
            return self.op("act", lambda e: e.activation(out=out, in_=in_, func=AF.Copy, scale=scale), r=r, w=w)
        if self._ev % 2 == 0:
            return self.copy("act", out, in_, r, w)
        return self.copy("dve", out, in_, r, w)

    def xattn(self, l, nxt=None):
        S_ = self.S
        self.rmsnorm_x(("xattn_norm", l))
        o = self.A0
        S_.region("QT", o, 8 * S * 2); QT = self.view(o, [128, 8, S], BF16); o += 8 * S * 2
        S_.region("MNT", o, 8 * MEM * 2); MNT = self.view(o, [128, 8, MEM], BF16); o += 8 * MEM * 2
        S_.region("KT", o, 8 * MEM * 2); KT = self.view(o, [128, 8, MEM], BF16); o += 8 * MEM * 2
        S_.region("V", o, 2 * 1024 * 2); V = self.view(o, [128, 2, 1024], BF16); o += 2 * 1024 * 2
        oWV = o
        S_.region("WV", o, 8 * 1024 * 2); WV = self.view(o, [128, 8, 1024], BF16); o += 8 * 1024 * 2
        o = self.wring("W", o, 4)
        S_.region("PT", o, 4 * TT * 2); PT = self.view(o, [128, 4, TT], BF16); o += 4 * TT * 2
        S_.region("RD", o, 2 * TT * 4); RD = self.view(o, [128, 2, TT], F32); o += 2 * TT * 4
        assert o <= self.AEND
        for c in range(8):
            self.stt("dve", MNT[:, c, :], self.MEMT[:, c, :], self.g(("mem_norm", l), c), self.MRS,
                     ALU.mult, ALU.mult, r=[("MEMT",), ("MRS",), ("SMALL",)], w=[("MNT", c)])
        mk = [("MNT", c) for c in range(8)]
        self.dma_w(WV.rearrange("p k n -> p (k n)"), self.xa_wv[l], [("WV",)])
        for oc in range(8):
            W, wk = self.wload(self.xa_wq[l, oc])
            for t in range(NT):
                ts = slice(t * TT, (t + 1) * TT)
                b = self.bank()
                self.mm_group(self.ps[:, b, :], [(W[:, k * 128:(k + 1) * 128], self.HT[:, k, ts]) for k in range(8)],
                              r=[("HT", k, t) for k in range(8)] + [wk], w=[("ps", b)])
                self.evac_copy(QT[:, oc, ts], self.ps[:, b, :], r=[("ps", b)], w=[("QT", oc, t)], scale=1.0 / 16.0)
        for oc in range(8):
            W, wk = self.wload(self.xa_wk[l, oc])
            b = self.bank()
            self.mm_group(self.ps[:, b, 0:MEM], [(W[:, k * 128:(k + 1) * 128], MNT[:, k, :]) for k in range(8)],
                          r=mk + [wk], w=[("ps", b)])
            self.evac_copy(KT[:, oc, :], self.ps[:, b, 0:MEM], r=[("ps", b)], w=[("KT", oc)])
        for mt in range(2):
            for nh in range(2):
                b = self.bank()
                self.mm_group(self.ps[:, b, :],
                              [(MNT[:, k, mt * 128:(mt + 1) * 128], WV[:, k, nh * 512:(nh + 1) * 512]) for k in range(8)],
                              r=mk + [("WV",)], w=[("ps", b)])
                self.evac_copy(V[:, mt, nh * 512:(nh + 1) * 512], self.ps[:, b, :], r=[("ps", b)], w=[("V", mt, nh)])
        W8 = self.out_proj_load(oWV, lambda oc: self.xa_wo[l, oc])
        pt_i = 0
        rd_i = 0
        items = [(h, t) for h in range(4) for t in range(NT)]
        sbanks = [0, 1, 2, 3]
        sb_i = 0
        obanks = [4, 5, 6, 7]
        ob_i = 0
        pend = {}

        def scores(h, t):
            nonlocal sb_i, pt_i
            ts = slice(t * TT, (t + 1) * TT)
            slots = []
            for mt in range(2):
                bs = sbanks[sb_i % 4]; sb_i += 1
                self.mm_group(self.ps[:, bs, :],
                              [(KT[:, 2 * h + dc, mt * 128:(mt + 1) * 128], QT[:, 2 * h + dc, ts]) for dc in range(2)],
                              r=[("KT", 2 * h), ("KT", 2 * h + 1), ("QT", 2 * h, t), ("QT", 2 * h + 1, t)], w=[("ps", bs)])
                sl = pt_i % 4
                pt_i += 1
                self.act(PT[:, sl, :], self.ps[:, bs, :], AF.Exp, r=[("ps", bs)], w=[("PT", sl)])
                slots.append(sl)
            pend[(h, t)] = slots

        def pv(h, t):
            nonlocal ob_i, rd_i
            ts = slice(t * TT, (t + 1) * TT)
            slots = pend.pop((h, t))
            bd = obanks[ob_i % 4]; ob_i += 1
            self.mm_group(self.ps[:, bd, :], [(self.ONES, PT[:, sl, :]) for sl in slots],
                          r=[("PT", sl) for sl in slots] + [("ONES",)], w=[("ps", bd)])
            rj = rd_i % 2
            rd_i += 1
            self.recip_act(RD[:, rj, :], self.ps[:, bd, :], r=[("ps", bd)], w=[("RD", rj)])
            for dc in range(2):
                c = 2 * h + dc
                bo = obanks[ob_i % 4]; ob_i += 1
                self.mm_group(self.ps[:, bo, :],
                              [(V[:, mt, c * 128:(c + 1) * 128], PT[:, slots[mt], :]) for mt in range(2)],
                              r=[("PT", sl) for sl in slots] + [("V", mt, c // 4) for mt in range(2)], w=[("ps", bo)])
                self.tt("dve", self.HT[:, c, ts], self.ps[:, bo, :], RD[:, rj, :], ALU.mult,
                        r=[("ps", bo), ("RD", rj)], w=[("HT", c, t)])
        for n_, it in enumerate(items):
            scores(*it)
            if n_ >= 1:
                pv(*items[n_ - 1])
        pv(*items[-1])
        self.out_proj(W8, nxt)

    def out_proj_load(self, off, wsrc):
        self.S.region("W8", off, 8 * 2048)
        W8 = self.view(off, [128, 8, 1024], BF16)
        for oc in range(8):
            self.dma_w(W8[:, oc, :], wsrc(oc), [("W8", oc)])
        return W8

    def out_proj(self, W8, nxt=None):
        for t in range(NT):
            ts = slice(t * TT, (t + 1) * TT)
            for oc in range(8):
                b = self.bank()
                self.mm_group(self.ps[:, b, :], [(W8[:, oc, k * 128:(k + 1) * 128], self.HT[:, k, ts]) for k in range(8)],
                              r=[("HT", k, t) for k in range(8)] + [("W8", oc)], w=[("ps", b)])
                self.tt("dve", self.XT[:, oc, ts], self.ps[:, b, :], self.XT[:, oc, ts], ALU.add,
                        r=[("ps", b), ("XT", oc, t)], w=[("XT", oc, t)])
            if nxt is not None and t >= 1:
                self.norm_tile(nxt, t - 1)
        if nxt is not None:
            self.norm_tile(nxt, NT - 1)
            self.pre_norm = nxt

    def mixer_ab(self, i, l, nxt=None):
        S_ = self.S
        self.rmsnorm_x(("mix_norm", l))
        A0 = self.A0
        PADL = 16
        HPW = S + 32
        o = A0
        S_.region("UF", o, 4 * S * 2); UF = self.view(o, [128, 4, S], BF16); o += 4 * S * 2
        S_.region("HP", o, 4 * HPW * 2); HP = self.view(o, [128, 4, HPW], BF16); o += 4 * HPW * 2
        oB = o
        o = self.wring("W", o, 4)
        S_.region("SGT", o, 2 * TT * 4); SGT = self.view(o, [128, 2, TT], F32); o += 2 * TT * 4
        assert o <= self.AEND
        for ch in range(4):
            self.memset("dve", HP[:, ch, 0:PADL], 0.0, w=[("HP", ch, "padl")])
            self.memset("dve", HP[:, ch, PADL + S:HPW], 0.0, w=[("HP", ch, "padr")])
        for oc in range(4):
            W, wk = self.wload(self.ab_win[i, oc])
            for t in range(NT):
                ts = slice(t * TT, (t + 1) * TT)
                b = self.bank()
                self.mm_group(self.ps[:, b, :], [(W[:, k * 128:(k + 1) * 128], self.HT[:, k, ts]) for k in range(8)],
                              r=[("HT", k, t) for k in range(8)] + [wk], w=[("ps", b)])
                self.evac_copy(UF[:, oc, ts], self.ps[:, b, :], r=[("ps", b)], w=[("UF", oc, t)])
        sg_i = 0
        for ch in range(4):
            Wa, wka = self.wload(self.ab_win[i, 4 + ch])
            Wg, wkg = self.wload(self.ab_win[i, 8 + ch])
            for t in range(NT):
                ts = slice(t * TT, (t + 1) * TT)
                ba = self.bank()
                bg = self.bank()
                hk = [("HT", k, t) for k in range(8)]
                self.mm_group(self.ps[:, ba, :], [(Wa[:, k * 128:(k + 1) * 128], self.HT[:, k, ts]) for k in range(8)],
                              r=hk + [wka], w=[("ps", ba)])
                self.mm_group(self.ps[:, bg, :], [(Wg[:, k * 128:(k + 1) * 128], self.HT[:, k, ts]) for k in range(8)],
                              r=hk + [wkg], w=[("ps", bg)])
                sg = sg_i % 2
                sg_i += 1
                self.act(SGT[:, sg, :], self.ps[:, bg, :], AF.Sigmoid, r=[("ps", bg)], w=[("SGT", sg)])
                self.tt("dve", HP[:, ch, PADL + t * TT:PADL + (t + 1) * TT], SGT[:, sg, :], self.ps[:, ba, :], ALU.mult,
                        r=[("SGT", sg), ("ps", ba)], w=[("HP", ch, t)])
        o = oB
        S_.region("AB", o, 16 * 4 * 256 * 2); AB = self.view(o, [128, 16, 4, 256], BF16); o += 16 * 4 * 256 * 2
        NCS = 2
        S_.region("CS", o, NCS * 8 * 512 * 2); CS = self.view(o, [128, NCS, 8, 512], BF16); o += NCS * 8 * 512 * 2
        oDG0 = o
        S_.region("DG0", o, CONVW * 128 * 2); DG0 = self.view(o, [128, CONVW, 128], BF16); o += CONVW * 128 * 2
        assert o <= self.AEND

        def build_diag(DGv, key, ch):
            for j in range(CONVW):
                self.ts_("dve", DGv[:, j, :], self.IDENT, self.g(("ab_conv_w", i), j * 4 + ch), None, ALU.mult, ALU.bypass,
                         r=[("IDENT",), ("SMALL",)], w=[key])
        for tt_ in range(16):
            for gp in range(2):
                b = self.bank()
                for gg in range(2):
                    g_ = 2 * gp + gg
                    self.mm_group(self.ps[:, b, gg * 256:(gg + 1) * 256],
                                  [(UF[:, g_, tt_ * 128:(tt_ + 1) * 128], self.DFTC)],
                                  r=[("UF", g_, tt_ // 4), ("DFTC",)], w=[("ps", b)])
                self.evac_copy(AB[:, tt_, 2 * gp:2 * gp + 2, :].rearrange("p a b -> p (a b)"), self.ps[:, b, :],
                               r=[("ps", b)], w=[("AB", tt_, gp)])
        W8 = self.out_proj_load(A0, lambda oc: self.ab_wout[i, oc])
        build_diag(DG0, ("DG0",), 0)
        cs_i = 0
        for st in range(4):
            banks = [self.bank() for _ in range(4)]
            npiece = 4
            for pc in range(npiece):
                cs = pc // 2
                half = pc % 2
                sl = cs_i % NCS
                cs_i += 1
                self.dma_sp(CS[:, sl, :, :].rearrange("p a b -> p (a b)"),
                            self.dfts_d[st, cs, :, half * 8 * 512:(half + 1) * 8 * 512], w=[("CS", sl)])
                for g_ in range(4):
                    b = banks[g_]
                    pairs = [(AB[:, half * 8 + s8, g_, cs * 128:(cs + 1) * 128], CS[:, sl, s8, :]) for s8 in range(8)]
                    self.mm_acc(self.ps[:, b, :], pairs, first=(pc == 0), last=(pc == npiece - 1),
                                r=[("AB", half * 8 + s8, g_ // 2) for s8 in range(8)] + [("CS", sl)], w=[("ps", b)])
            for g_ in range(4):
                self.evac_copy(self.HT[:, g_, st * TT:(st + 1) * TT], self.ps[:, banks[g_], :],
                               r=[("ps", banks[g_])], w=[("HT", g_, st)])
        o = oB
        S_.region("DG1", o, CONVW * 128 * 2); DG1 = self.view(o, [128, CONVW, 128], BF16); o += CONVW * 128 * 2
        S_.region("CO", o, 4 * S * 4); CO = self.view(o, [128, 4, S], F32); o += 4 * S * 4
        assert o <= oDG0
        for ch in range(4):
            DGv, dkey = (DG0, ("DG0",)) if ch % 2 == 0 else (DG1, ("DG1",))
            if ch > 0:
                build_diag(DGv, dkey, ch)
            for t in range(NT):
                ts = slice(t * TT, (t + 1) * TT)
                b = self.bank()
                self.mm_group(self.ps[:, b, :],
                              [(DGv[:, j, :], HP[:, ch, t * TT + j + 1:t * TT + j + 1 + TT]) for j in range(CONVW)],
                              r=[dkey, ("HP", ch, "padl"), ("HP", ch, "padr")] + [("HP", ch, tq) for tq in range(NT)],
                              w=[("ps", b)])
                self.act(CO[:, ch, ts], self.ps[:, b, :], AF.Identity, r=[("ps", b), ("SMALL",)], w=[("CO", ch, t)],
                         bias=self.g(("ab_conv_b", i), ch))
        self.ps_rr = 0
        lnb = []
        for t in range(NT):
            ts = slice(t * TT, (t + 1) * TT)
            bm = self.bank()
            bq = self.bank()
            lnb.append((bm, bq))
            for ch in range(4):
                i1 = self.sq_i; self.sq_i = (i1 + 1) % 4
                self.act(self.SQ[:, i1, :], CO[:, ch, ts], AF.Square, r=[("CO", ch, t)], w=[("SQ", i1)])
                self.op("pe", self._mm1(self.ps[:, bq, :], self.INV[512], self.SQ[:, i1, :], ch == 0, ch == 3),
                        r=[("SQ", i1), ("INV", 512)], w=[("ps", bq)])
                i2 = self.sq_i; self.sq_i = (i2 + 1) % 4
                self.copy("dve", self.SQ[:, i2, :], CO[:, ch, ts], r=[("CO", ch, t)], w=[("SQ", i2)])
                self.op("pe", self._mm1(self.ps[:, bm, :], self.INV[512], self.SQ[:, i2, :], ch == 0, ch == 3),
                        r=[("SQ", i2), ("INV", 512)], w=[("ps", bm)])
        for t in range(NT):
            bm, bq = lnb[t]
            j = self.rs_i; self.rs_i = (j + 1) % 2
            rs = self.RS[:, j, :]
            self.act(rs, self.ps[:, bm, :], AF.Square, r=[("ps", bm)], w=[("RS", j)])
            self.tt("dve", rs, self.ps[:, bq, :], rs, ALU.subtract, r=[("ps", bq), ("RS", j)], w=[("RS", j)])
            self.act(rs, rs, AF.Ln, r=[("RS", j)], w=[("RS", j)], bias=EPS, scale=1.0)
            self.act(self.ps[:, bq, :], rs, AF.Exp, r=[("RS", j)], w=[("ps", bq)], scale=-0.5)
        for t in range(NT):
            ts = slice(t * TT, (t + 1) * TT)
            bm, bq = lnb[t]
            for ch in range(4):
                self.tt("dve", CO[:, ch, ts], CO[:, ch, ts], self.ps[:, bm, :], ALU.subtract,
                        r=[("CO", ch, t), ("ps", bm)], w=[("CO", ch, t)])
                self.tt("dve", CO[:, ch, ts], CO[:, ch, ts], self.ps[:, bq, :], ALU.mult,
                        r=[("CO", ch, t), ("ps", bq)], w=[("CO", ch, t)])
                self.act(self.HT[:, 4 + ch, ts], CO[:, ch, ts], AF.Silu, r=[("CO", ch, t), ("SMALL",)], w=[("HT", 4 + ch, t)],
                         bias=self.g(("ab_conv_ln_b", i), ch), scale=self.g(("ab_conv_ln_g", i), ch))
        self.out_proj(W8, nxt)

    def _mm1(self, out, lhsT, rhs, start, stop):
        return lambda e: e.matmul(out, lhsT, rhs, start=start, stop=stop)

    def mm_acc(self, out, pairs, first, last, r, w):
        n = len(pairs)

        def fn(e):
            bi = None
            for i_, (l_, r_) in enumerate(pairs):
                bi = e.matmul(out, l_, r_, start=(first and i_ == 0), stop=(last and i_ == n - 1))
            return bi
        return self.op("pe", fn, r=r, w=w)

    def mixer_mla(self, i, l, nxt=None):
        S_ = self.S
        self.rmsnorm_x(("mix_norm", l))
        o = self.A0
        oCQN = o
        S_.region("CQN", o, 4 * S * 2); CQN = self.view(o, [128, 4, S], BF16); o += 4 * S * 2
        S_.region("CKVN", o, 2 * S * 2); CKVN = self.view(o, [128, 2, S], BF16); o += 2 * S * 2
        S_.region("KR", o, S * 2); KR = self.view(o, [128, S], BF16); o += S * 2
        S_.region("ROPE", o, 2 * S * 4); ROPE = self.view(o, [128, 2, S], F32); o += 2 * S * 4
        S_.region("T12", o, 2 * TT * 4); T12 = self.view(o, [128, 2, TT], F32); o += 2 * TT * 4
        oP = o
        S_.region("WIN", o, 8 * 2048); WIN = self.view(o, [128, 8, 1024], BF16); o += 8 * 2048
        S_.region("CQ32", o, 4 * TT * 4); CQ32 = self.view(o, [128, 4, TT], F32); o += 4 * TT * 4
        S_.region("CKV32", o, 2 * TT * 4); CKV32 = self.view(o, [128, 2, TT], F32); o += 2 * TT * 4
        assert o <= self.AEND
        self.dma_sp(ROPE[:, 0, :], self.rope_d[0], w=[("ROPE", 0)])
        self.dma_sp(ROPE[:, 1, :], self.rope_d[1], w=[("ROPE", 1)])
        for oc in range(8):
            self.dma_w(WIN[:, oc, :], self.mla_win[i, oc], [("WIN", oc)])

        def rope_combine(dst, ba, bb, ts, wkey):
            self.tt("dve", T12[:, 0, :], self.ps[:, ba, :], ROPE[:, 0, ts], ALU.mult,
                    r=[("ps", ba), ("ROPE", 0)], w=[("T12", 0)])
            self.tt("dve", T12[:, 1, :], self.ps[:, bb, :], ROPE[:, 1, ts], ALU.mult,
                    r=[("ps", bb), ("ROPE", 1)], w=[("T12", 1)])
            self.tt("dve", dst, T12[:, 0, :], T12[:, 1, :], ALU.add,
                    r=[("T12", 0), ("T12", 1)], w=[wkey])

        for t in range(NT):
            ts = slice(t * TT, (t + 1) * TT)
            hk = [("HT", k, t) for k in range(8)]

            def proj(oc):
                b = self.bank()
                self.mm_group(self.ps[:, b, :], [(WIN[:, oc, k * 128:(k + 1) * 128], self.HT[:, k, ts]) for k in range(8)],
                              r=hk + [("WIN", oc)], w=[("ps", b)])
                return b
            for oc in range(4):
                b = proj(oc)
                self.copy("act", CQ32[:, oc, :], self.ps[:, b, :], r=[("ps", b)], w=[("CQ32", oc)])
            b = self.norm_stats([(CQ32[:, oc, :], ("CQ32", oc)) for oc in range(4)], 512)
            for oc in range(4):
                self.stt("dve", CQN[:, oc, ts], CQ32[:, oc, :], self.g(("mla_q_norm", i), oc), self.ps[:, b, :],
                         ALU.mult, ALU.mult, r=[("CQ32", oc), ("ps", b), ("SMALL",)], w=[("CQN", oc, t)])
            for oc in range(2):
                b = proj(4 + oc)
                self.copy("act", CKV32[:, oc, :], self.ps[:, b, :], r=[("ps", b)], w=[("CKV32", oc)])
            b = self.norm_stats([(CKV32[:, oc, :], ("CKV32", oc)) for oc in range(2)], 256)
            for oc in range(2):
                self.stt("dve", CKVN[:, oc, ts], CKV32[:, oc, :], self.g(("mla_kv_norm", i), oc), self.ps[:, b, :],
                         ALU.mult, ALU.mult, r=[("CKV32", oc), ("ps", b), ("SMALL",)], w=[("CKVN", oc, t)])
            ba = proj(6)
            bb = proj(7)
            rope_combine(KR[:, ts], ba, bb, ts, ("KR", t))

        o = oP
        S_.region("QN", o, S * 2); QN = self.view(o, [128, S], BF16); o += S * 2
        S_.region("QR", o, S * 2); QR = self.view(o, [128, S], BF16); o += S * 2
        S_.region("KN", o, S * 2); KN = self.view(o, [128, S], BF16); o += S * 2
        S_.region("VH", o, S * 2); VH = self.view(o, [128, 16, 128], BF16); o += S * 2
        S_.region("WQ", o, 2 * 3072); WQ = self.view(o, [128, 2, 1536], BF16); o += 2 * 3072
        S_.region("WKV", o, 2 * 1024); WKV = self.view(o, [128, 2, 512], BF16); o += 2 * 1024
        NPT = 6
        S_.region("PT", o, NPT * TT * 2); PT = self.view(o, [128, NPT, TT], BF16); o += NPT * TT * 2
        S_.region("RD", o, 2 * TT * 4); RD = self.view(o, [128, 2, TT], F32); o += 2 * TT * 4
        oW = o
        assert o <= self.AEND
        SCALE = 1.0 / math.sqrt(192.0)
        pj = [7, 0, 1]
        pj_i = 0
        sb = [0, 1, 2]
        sb_i = 0
        DO = [(3, 4), (5, 6)]
        do_i = 0
        pt_i = 0
        rd_i = 0
        LA = 2
        for h in range(8):
            s = h % 2
            self.dma_w(WQ[:, s, :], self.mla_wq[i, h], [("WQ", s)])
            self.dma_w(WKV[:, s, :], self.mla_wkv[i, h], [("WKV", s)])
            for t in range(NT):
                ts = slice(t * TT, (t + 1) * TT)
                cq = [("CQN", k, t) for k in range(4)]
                b = pj[pj_i % 3]; pj_i += 1
                self.mm_group(self.ps[:, b, :], [(WQ[:, s, k * 384:k * 384 + 128], CQN[:, k, ts]) for k in range(4)],
                              r=cq + [("WQ", s)], w=[("ps", b)])
                self.copy("act", QN[:, ts], self.ps[:, b, :], r=[("ps", b)], w=[("QN", t)])
                ba = pj[pj_i % 3]; pj_i += 1
                self.mm_group(self.ps[:, ba, :], [(WQ[:, s, k * 384 + 128:k * 384 + 256], CQN[:, k, ts]) for k in range(4)],
                              r=cq + [("WQ", s)], w=[("ps", ba)])
                bb = pj[pj_i % 3]; pj_i += 1
                self.mm_group(self.ps[:, bb, :], [(WQ[:, s, k * 384 + 256:k * 384 + 384], CQN[:, k, ts]) for k in range(4)],
                              r=cq + [("WQ", s)], w=[("ps", bb)])
                rope_combine(QR[:, ts], ba, bb, ts, ("QR", t))
                b = pj[pj_i % 3]; pj_i += 1
                self.mm_group(self.ps[:, b, :], [(WKV[:, s, k * 256:k * 256 + 128], CKVN[:, k, ts]) for k in range(2)],
                              r=[("CKVN", k, t) for k in range(2)] + [("WKV", s)], w=[("ps", b)])
                self.copy("act", KN[:, ts], self.ps[:, b, :], r=[("ps", b)], w=[("KN", t)])
            for kt4 in range(4):
                b = pj[pj_i % 3]; pj_i += 1
                for q in range(4):
                    kt = kt4 * 4 + q
                    self.mm_group(self.ps[:, b, q * 128:(q + 1) * 128],
                                  [(CKVN[:, k, kt * 128:(kt + 1) * 128], WKV[:, s, k * 256 + 128:k * 256 + 256]) for k in range(2)],
                                  r=[("CKVN", k, kt4) for k in range(2)] + [("WKV", s)], w=[("ps", b)])
                self.copy("dve", VH[:, kt4 * 4:(kt4 + 1) * 4, :].rearrange("p a b -> p (a b)"), self.ps[:, b, :],
                          r=[("ps", b)], w=[("VH", kt4)])
            if h == 7:
                W8 = self.out_proj_load(oCQN, lambda oc: self.mla_wout[i, oc])
            items = [(t, kt) for t in range(NT) for kt in range(16)]
            slots = {}
            for n_ in range(len(items) + LA):
                if n_ < len(items):
                    t, kt = items[n_]
                    ts = slice(t * TT, (t + 1) * TT)
                    bs = sb[sb_i % 3]; sb_i += 1
                    self.mm_group(self.ps[:, bs, :],
                                  [(KN[:, kt * 128:(kt + 1) * 128], QN[:, ts]), (KR[:, kt * 128:(kt + 1) * 128], QR[:, ts])],
                                  r=[("KN", kt // 4), ("QN", t), ("KR", kt // 4), ("QR", t)], w=[("ps", bs)])
                    sl = pt_i % NPT; pt_i += 1
                    slots[(t, kt)] = sl
                    self.act(PT[:, sl, :], self.ps[:, bs, :], AF.Exp, r=[("ps", bs)], w=[("PT", sl)], scale=SCALE)
                if n_ >= LA:
                    t, kt = items[n_ - LA]
                    ts = slice(t * TT, (t + 1) * TT)
                    sl = slots.pop((t, kt))
                    BD, BO = DO[do_i % 2]
                    self.mm_acc(self.ps[:, BD, :], [(self.ONES, PT[:, sl, :])], first=(kt == 0), last=(kt == 15),
                                r=[("PT", sl), ("ONES",)], w=[("ps", BD)])
                    self.mm_acc(self.ps[:, BO, :], [(VH[:, kt, :], PT[:, sl, :])], first=(kt == 0), last=(kt == 15),
                                r=[("PT", sl), ("VH", kt // 4)], w=[("ps", BO)])
                    if kt == 15:
                        do_i += 1
                        rj = rd_i % 2; rd_i += 1
                        self.recip_act(RD[:, rj, :], self.ps[:, BD, :], r=[("ps", BD)], w=[("RD", rj)])
                        self.tt("dve", self.HT[:, h, ts], self.ps[:, BO, :], RD[:, rj, :], ALU.mult,
                                r=[("ps", BO), ("RD", rj)], w=[("HT", h, t)])
        self.ps_rr = 0
        self.out_proj(W8, nxt)


def _run(inputs, nseq=SEQ_PER_CORE, layers=DEPTH, parts=("ffn1", "mix", "xattn", "ffn2"), ncores=NCORES, trace=False):
    xs = np.concatenate([np.asarray(inputs["x_prompt"], np.float32), np.asarray(inputs["x_sample"], np.float32)], axis=0)
    ms = np.concatenate([np.asarray(inputs["mem_prompt"], np.float32), np.asarray(inputs["mem_sample"], np.float32)], axis=0)
    w = prep_weights(inputs)
    w.update(make_consts())
    b = Builder(nseq=nseq, layers=layers, parts=parts)
    nc = b.build()
    in_maps = []
    for c in range(ncores):
        sl = slice(c * nseq, (c + 1) * nseq)
        m = dict(w)
        m["xT"] = np.ascontiguousarray(xs[sl].transpose(0, 2, 1))
        m["memT"] = np.ascontiguousarray(ms[sl].transpose(0, 2, 1))
        in_maps.append(m)
    res = run_bass_kernel_spmd(nc, in_maps, core_ids=list(range(ncores)), trace=trace)
    ys = np.concatenate([np.asarray(r["yT"]).transpose(0, 2, 1) for r in res.results], axis=0)
    return np.ascontiguousarray(ys.astype(np.float32)), res


def kernel(**inputs):
    ys, _ = _run(inputs)
    nb = inputs["x_prompt"].shape[0]
    return (ys[:nb], ys[nb:])
```

```python
import math
import numpy as np
import ml_dtypes
import concourse.bass as bass
import concourse.mybir as mybir
from concourse.bass_utils import run_bass_kernel_spmd

F32 = mybir.dt.float32
BF16 = mybir.dt.bfloat16
U8 = mybir.dt.uint8
AF = mybir.ActivationFunctionType
ALU = mybir.AluOpType

D = 1024
S = 2048
DEPTH = 4
MEM = 256
DFF = 2816
NF = DFF // 128
NT = 4
TT = 512
EPS = 1e-6
NCORES = 8
SEQ_PER_CORE = 3
CONVW = 31
ENGS = ("pe", "act", "dve", "pool", "sp")


class _Ins:
    __slots__ = ("eng", "fn", "deps", "dma", "sig", "sem", "val", "waits", "know", "idx")


class Sched:
    NDS = 8

    def __init__(self):
        self.streams = {e: [] for e in ENGS}
        self.order = []
        self.lastw = {}
        self.readers = {}
        self.regions = {}
        self.rlist = []
        self.uid = 0
        self.cur = {}
        self.region_keys = {}
        self.touched = set()
        self.epoch_marks = []

    def _kill(self, name):
        own = set()
        for k in self.region_keys.pop(name, ()):
            lw = self.lastw.pop(k, None)
            if lw is not None:
                own.add(lw)
            rd = self.readers.pop(k, None)
            if rd:
                own.update(rd[0].values())
                own.update(rd[1])
            self.touched.discard(k)
        if not own:
            return None
        best = {}
        fence = set()
        for ins in own:
            if ins.dma:
                fence.add(ins)
            else:
                cur = best.get(ins.eng)
                if cur is None or cur.idx < ins.idx:
                    best[ins.eng] = ins
        fence.update(best.values())
        return fence

    def region(self, base, off, size):
        self.uid += 1
        name = "%s#%d" % (base, self.uid)
        self.cur[base] = name
        end = off + size
        inherited = set()
        newlist = []
        for ent in self.rlist:
            if not any(s < end and off < e_ for (s, e_) in ent["segs"]):
                newlist.append(ent)
                continue
            if ent["alive"]:
                f = self._kill(ent["name"])
                ent["alive"] = False
                if f is not None:
                    ent["fence"] = f
                self.regions.pop(ent["name"], None)
            inherited.update(ent["fence"])
            segs = []
            for (s, e_) in ent["segs"]:
                if s < off:
                    segs.append((s, min(e_, off)))
                if e_ > end:
                    segs.append((max(s, end), e_))
            segs = [(s, e_) for (s, e_) in segs if e_ > s]
            if segs:
                ent["segs"] = segs
                newlist.append(ent)
        ent = {"name": name, "segs": [(off, end)], "fence": inherited, "alive": True}
        newlist.append(ent)
        self.rlist = newlist
        self.regions[name] = (off, size, inherited)
        self.region_keys[name] = set()

    def new_epoch(self):
        self.epoch_marks.append(len(self.order))

    def op(self, eng, fn, r=(), w=(), dma=False):
        ins = _Ins()
        ins.eng = eng
        ins.fn = fn
        ins.dma = dma
        ins.sig = dma
        ins.idx = len(self.order)
        cur = self.cur
        r = [((cur[k[0]],) + tuple(k[1:])) if k[0] in cur else k for k in r]
        w = [((cur[k[0]],) + tuple(k[1:])) if k[0] in cur else k for k in w]
        deps = set()
        for k in tuple(r) + tuple(w):
            if k not in self.touched:
                self.touched.add(k)
                reg = self.regions.get(k[0])
                if reg is not None:
                    deps.update(reg[2])
                    self.region_keys[k[0]].add(k)
        for k in r:
            lw = self.lastw.get(k)
            if lw is not None:
                deps.add(lw)
        for k in w:
            lw = self.lastw.get(k)
            if lw is not None:
                deps.add(lw)
            rd = self.readers.get(k)
            if rd:
                deps.update(rd[0].values())
                deps.update(rd[1])
        for k in r:
            rd = self.readers.get(k)
            if rd is None:
                rd = self.readers[k] = ({}, [])
            if dma:
                rd[1].append(ins)
            else:
                rd[0][eng] = ins
        for k in w:
            self.lastw[k] = ins
            self.readers[k] = ({}, [])
        deps.discard(ins)
        if eng == "pe":
            deps = {d for d in deps if d.eng != "pe"}
        ins.deps = deps
        self.streams[eng].append(ins)
        self.order.append(ins)
        return ins

    def finalize(self):
        for ins in self.order:
            for d in ins.deps:
                d.sig = True
        marks = set(self.epoch_marks)
        nsem = 0
        engsem = {}
        count = {}

        def fresh():
            nonlocal nsem
            for e in ENGS:
                engsem[e] = nsem
                nsem += 1
                count[e] = 0
        fresh()
        dmasem = {}
        for q in ("sp", "pool"):
            dmasem[q] = list(range(nsem, nsem + self.NDS))
            nsem += self.NDS
        dmacount = {"sp": 0, "pool": 0}
        dmaprev = {"sp": [None] * self.NDS, "pool": [None] * self.NDS}
        know = {e: {} for e in ENGS}
        self.final_waits = {}
        for idx, ins in enumerate(self.order):
            if idx in marks:
                fresh()
            E = ins.eng
            kn = know[E]
            deps = ins.deps
            if ins.dma:
                i = dmacount[E]
                slot = i % self.NDS
                prev = dmaprev[E][slot]
                if prev is not None:
                    deps = set(deps)
                    deps.add(prev)
            waits = []
            dl = sorted(deps, key=lambda d: (d.sem, -d.val))
            for d in dl:
                if kn.get(d.sem, 0) >= d.val:
                    continue
                waits.append((d.sem, d.val))
                for s_, v_ in d.know.items():
                    if kn.get(s_, 0) < v_:
                        kn[s_] = v_
            ins.waits = waits
            if ins.dma:
                ins.sem = dmasem[E][slot]
                ins.val = 16 * (i // self.NDS + 1)
                dmaprev[E][slot] = ins
                dmacount[E] = i + 1
                ins.know = dict(kn)
                ins.know[ins.sem] = ins.val
                self.final_waits[ins.sem] = ins.val
            elif ins.sig:
                count[E] += 1
                ins.sem = engsem[E]
                ins.val = count[E]
                ins.know = dict(kn)
                ins.know[ins.sem] = ins.val
            else:
                ins.sem = -1
                ins.val = 0
                ins.know = None
            ins.deps = None
        self.nsem = nsem
        return nsem

    def emit(self, name, e, sems, final=False):
        for ins in self.streams[name]:
            for (s_, v_) in ins.waits:
                e.wait_ge(sems[s_], v_)
            bi = ins.fn(e)
            if ins.dma:
                bi.then_inc(sems[ins.sem], 16)
            elif ins.sig:
                bi.then_inc(sems[ins.sem], 1)
        if final:
            for s_, v_ in sorted(self.final_waits.items()):
                e.wait_ge(sems[s_], v_)


def _small_layout():
    off = {}
    n = 0

    def add(name, cols):
        nonlocal n
        off[name] = n
        n += cols
    for l in range(DEPTH):
        for nm in ("ffn1_norm", "mix_norm", "xattn_norm", "mem_norm", "ffn2_norm"):
            add((nm, l), 8)
    add(("final_norm", 0), 8)
    for i in range(2):
        add(("mla_q_norm", i), 4)
        add(("mla_kv_norm", i), 2)
        add(("ab_conv_b", i), 4)
        add(("ab_conv_ln_g", i), 4)
        add(("ab_conv_ln_b", i), 4)
        add(("ab_conv_w", i), CONVW * 4)
    return off, n


SMALL_OFF, NSMALL = _small_layout()


def _fm(vec):
    v = np.asarray(vec, dtype=np.float32)
    return np.ascontiguousarray(v.reshape(-1, 128).T)


def _tile_lhsT(W):
    K, M = W.shape
    a = W.reshape(K // 128, 128, M // 128, 128).transpose(2, 1, 0, 3)
    return np.ascontiguousarray(a.reshape(M // 128, 128, (K // 128) * 128))


def _rows_pk(W):
    K, N = W.shape
    a = W.reshape(K // 128, 128, N).transpose(1, 0, 2)
    return np.ascontiguousarray(a.reshape(128, (K // 128) * N))


def prep_weights(inp):
    out = {}
    small = np.zeros((128, NSMALL), np.float32)

    def put(key, arr):
        o = SMALL_OFF[key]
        small[:, o:o + arr.shape[1]] = arr
    for l in range(DEPTH):
        for nm in ("ffn1_norm", "mix_norm", "xattn_norm", "mem_norm", "ffn2_norm"):
            put((nm, l), _fm(inp[nm][l]))
    put(("final_norm", 0), _fm(inp["final_norm"]))
    for i in range(2):
        put(("mla_q_norm", i), _fm(inp["mla_q_norm"][i]))
        put(("mla_kv_norm", i), _fm(inp["mla_kv_norm"][i]))
        put(("ab_conv_b", i), _fm(inp["ab_conv_b"][i]))
        put(("ab_conv_ln_g", i), _fm(inp["ab_conv_ln_g"][i]))
        put(("ab_conv_ln_b", i), _fm(inp["ab_conv_ln_b"][i]))
        cw = np.asarray(inp["ab_conv_w"][i], np.float32)
        a = cw.reshape(CONVW, 4, 128).transpose(2, 0, 1).reshape(128, CONVW * 4)
        put(("ab_conv_w", i), a)
    out["small"] = small

    wgu = np.empty((DEPTH, 2, NF, 128, 2048), np.float32)
    wd = np.empty((DEPTH, 2, DFF, D), np.float32)
    for l in range(DEPTH):
        for wi, pre in enumerate(("ffn1", "ffn2")):
            g = _tile_lhsT(np.asarray(inp[pre + "_w_gate"][l], np.float32))
            u = _tile_lhsT(np.asarray(inp[pre + "_w_up"][l], np.float32))
            wgu[l, wi, :, :, :1024] = g
            wgu[l, wi, :, :, 1024:] = u
            wd[l, wi] = inp[pre + "_w_down"][l]
    out["ffn_wgu"] = wgu
    out["ffn_wd"] = wd

    xa_wq = np.empty((DEPTH, 8, 128, 1024), np.float32)
    xa_wk = np.empty((DEPTH, 8, 128, 1024), np.float32)
    xa_wv = np.empty((DEPTH, 128, 8 * 1024), np.float32)
    xa_wo = np.empty((DEPTH, 8, 128, 1024), np.float32)
    for l in range(DEPTH):
        xa_wq[l] = _tile_lhsT(np.asarray(inp["xattn_w_q"][l], np.float32))
        wkv = np.asarray(inp["xattn_w_kv"][l], np.float32)
        xa_wk[l] = _tile_lhsT(wkv[:, :1024])
        xa_wv[l] = _rows_pk(wkv[:, 1024:])
        xa_wo[l] = _tile_lhsT(np.asarray(inp["xattn_w_o"][l], np.float32))
    out["xa_wq"], out["xa_wk"], out["xa_wv"], out["xa_wo"] = xa_wq, xa_wk, xa_wv, xa_wo

    ab_win = np.empty((2, 12, 128, 1024), np.float32)
    ab_wout = np.empty((2, 8, 128, 1024), np.float32)
    mla_win = np.empty((2, 8, 128, 1024), np.float32)
    mla_wq = np.empty((2, 8, 128, 4 * 384), np.float32)
    mla_wkv = np.empty((2, 8, 128, 2 * 256), np.float32)
    mla_wout = np.empty((2, 8, 128, 1024), np.float32)
    for i in range(2):
        ab_win[i] = _tile_lhsT(np.asarray(inp["ab_w_in"][i], np.float32))
        ab_wout[i] = _tile_lhsT(np.asarray(inp["ab_w_out"][i], np.float32))
        win = np.asarray(inp["mla_w_in"][i], np.float32)
        kr = win[:, 768:832]
        krsw = np.concatenate([kr[:, 32:], kr[:, :32]], axis=1)
        z64 = np.zeros((1024, 64), np.float32)
        win_ext = np.concatenate([win[:, :768], kr, z64, krsw, z64], axis=1)
        mla_win[i] = _tile_lhsT(win_ext)
        wq = np.asarray(inp["mla_w_q_b"][i], np.float32).reshape(512, 8, 192)
        zq = np.zeros((512, 8, 64), np.float32)
        wq_ext = np.concatenate([wq[:, :, :128], wq[:, :, 128:192], zq,
                                 wq[:, :, 160:192], wq[:, :, 128:160], zq], axis=2)
        a = wq_ext.reshape(4, 128, 8, 384).transpose(2, 1, 0, 3)
        mla_wq[i] = a.reshape(8, 128, 4 * 384)
        wkv = np.asarray(inp["mla_w_kv_b"][i], np.float32).reshape(256, 8, 256)
        a = wkv.reshape(2, 128, 8, 256).transpose(2, 1, 0, 3)
        mla_wkv[i] = a.reshape(8, 128, 512)
        mla_wout[i] = _tile_lhsT(np.asarray(inp["mla_w_out"][i], np.float32))
    out["ab_win"], out["ab_wout"] = ab_win, ab_wout
    out["mla_win"], out["mla_wq"], out["mla_wkv"], out["mla_wout"] = mla_win, mla_wq, mla_wkv, mla_wout
    return out


def make_consts():
    c = {}
    n = np.arange(128, dtype=np.float64)
    ang = 2.0 * np.pi * np.outer(n, n) / 128.0
    dftc = np.concatenate([np.cos(ang), -np.sin(ang)], axis=1) / 512.0
    c["dftc"] = dftc.astype(ml_dtypes.bfloat16)
    s = np.arange(S, dtype=np.int64)
    prod = np.outer(s, s) % S
    angs = 2.0 * np.pi * prod.astype(np.float64) / S
    dfts = np.empty((4, 2, 128, 16 * 512), ml_dtypes.bfloat16)
    for cs, fn in enumerate((np.cos, np.sin)):
        m = fn(angs)
        m = m.reshape(16, 128, 4, 512).transpose(2, 1, 0, 3)
        dfts[:, cs] = m.reshape(4, 128, 16 * 512).astype(ml_dtypes.bfloat16)
    c["dfts"] = dfts
    inv_freq = (1.0 / (np.float32(10000.0) ** (np.arange(0, 64, 2, dtype=np.float32) / np.float32(64)))).astype(np.float32)
    angr = (np.arange(S, dtype=np.float32)[:, None] * inv_freq[None, :]).astype(np.float32)
    cos = np.cos(angr).astype(np.float32).T
    sin = np.sin(angr).astype(np.float32).T
    rope = np.zeros((2, 128, S), np.float32)
    rope[0, :32] = cos
    rope[0, 32:64] = cos
    rope[1, :32] = -sin
    rope[1, 32:64] = sin
    c["rope"] = rope
    c["ident"] = np.eye(128, dtype=np.float32).astype(ml_dtypes.bfloat16)
    return c


class Builder:
    def __init__(self, nseq=SEQ_PER_CORE, layers=DEPTH, parts=("ffn1", "mix", "xattn", "ffn2")):
        self.nseq = nseq
        self.layers = layers
        self.parts = parts
        self.nc = bass.Bass("TRN2", target_bir_lowering=False)
        self.S = Sched()
        self._uid = 0

    def dram_in(self, name, shape, dt=F32):
        return self.nc.dram_tensor(name, list(shape), dt, kind="ExternalInput").ap()

    def view(self, off, shape, dt):
        esz = 2 if dt == BF16 else 4
        n = 1
        for s_ in shape[1:]:
            n *= s_
        v = self.arena[0:shape[0], off:off + n * esz].bitcast(dt)
        if len(shape) == 3:
            v = v.rearrange("p (a b) -> p a b", a=shape[1])
        elif len(shape) == 4:
            v = v.rearrange("p (a b c) -> p a b c", a=shape[1], b=shape[2])
        return v

    def build(self):
        nc = self.nc
        S_ = self.S
        nseq = self.nseq
        self.xT = self.dram_in("xT", [nseq, D, S])
        self.memT = self.dram_in("memT", [nseq, D, MEM])
        self.small_d = self.dram_in("small", [128, NSMALL])
        self.ffn_wgu = self.dram_in("ffn_wgu", [DEPTH, 2, NF, 128, 2048])
        self.ffn_wd = self.dram_in("ffn_wd", [DEPTH, 2, DFF, D])
        self.xa_wq = self.dram_in("xa_wq", [DEPTH, 8, 128, 1024])
        self.xa_wk = self.dram_in("xa_wk", [DEPTH, 8, 128, 1024])
        self.xa_wv = self.dram_in("xa_wv", [DEPTH, 128, 8 * 1024])
        self.xa_wo = self.dram_in("xa_wo", [DEPTH, 8, 128, 1024])
        self.ab_win = self.dram_in("ab_win", [2, 12, 128, 1024])
        self.ab_wout = self.dram_in("ab_wout", [2, 8, 128, 1024])
        self.mla_win = self.dram_in("mla_win", [2, 8, 128, 1024])
        self.mla_wq = self.dram_in("mla_wq", [2, 8, 128, 1536])
        self.mla_wkv = self.dram_in("mla_wkv", [2, 8, 128, 512])
        self.mla_wout = self.dram_in("mla_wout", [2, 8, 128, 1024])
        self.dftc_d = self.dram_in("dftc", [128, 256], BF16)
        self.dfts_d = self.dram_in("dfts", [4, 2, 128, 16 * 512], BF16)
        self.rope_d = self.dram_in("rope", [2, 128, S])
        self.ident_d = self.dram_in("ident", [128, 128], BF16)
        self.yT = nc.dram_tensor("yT", [nseq, D, S], F32, kind="ExternalOutput").ap()

        ARENA = 212000
        self.arena = nc.alloc_sbuf_tensor("arena", [128, ARENA], U8)
        self.ps = nc.alloc_psum_tensor("ps", [128, 8, 512], F32)
        o = 0
        self.XT = self.view(o, [128, 8, S], F32); o += 8 * S * 4
        self.HT = self.view(o, [128, 8, S], BF16); o += 8 * S * 2
        self.SMALL = self.view(o, [128, NSMALL], F32); o += NSMALL * 4
        self.ONES = self.view(o, [128, 128], BF16); o += 256
        self.INV = {}
        for n_ in (1024, 512, 256):
            self.INV[n_] = self.view(o, [128, 128], BF16); o += 256
        self.IDENT = self.view(o, [128, 128], BF16); o += 256
        self.DFTC = self.view(o, [128, 256], BF16); o += 512
        self.MEMT = self.view(o, [128, 8, MEM], F32); o += 8 * MEM * 4
        self.SQ = self.view(o, [128, 4, TT], BF16); o += 4 * TT * 2
        self.RS = self.view(o, [128, 2, TT], F32); o += 2 * TT * 4
        self.MRS = self.view(o, [128, MEM], F32); o += MEM * 4
        o = (o + 63) // 64 * 64
        self.A0 = o
        self.AEND = ARENA
        self.sq_i = 0
        self.rs_i = 0
        self.ps_rr = 0

        self.prologue()
        for s_ in range(nseq):
            if s_ > 0:
                S_.new_epoch()
            self.sequence(s_)

        nsem = S_.finalize()
        sems = [nc.alloc_semaphore("s%d" % i) for i in range(nsem)]
        with nc.Block() as block:
            @block.tensor
            def _(e):
                S_.emit("pe", e, sems)

            @block.scalar
            def _(e):
                S_.emit("act", e, sems)

            @block.vector
            def _(e):
                S_.emit("dve", e, sems)

            @block.gpsimd
            def _(e):
                S_.emit("pool", e, sems)

            @block.sync
            def _(e):
                S_.emit("sp", e, sems, final=True)
        return nc

    def op(self, *a, **k):
        return self.S.op(*a, **k)

    def bank(self):
        b = self.ps_rr
        self.ps_rr = (b + 1) % 8
        return b

    def g(self, key, c):
        o = SMALL_OFF[key] + c
        return self.SMALL[:, o:o + 1]

    def dma_w(self, dst, src, wkeys):
        self.op("pool", lambda e: e.dma_start(out=dst, in_=src), r=(), w=wkeys, dma=True)

    def dma_sp(self, dst, src, r=(), w=()):
        self.op("sp", lambda e: e.dma_start(out=dst, in_=src), r=r, w=w, dma=True)

    def mm_group(self, out, pairs, r, w):
        n = len(pairs)

        def fn(e):
            bi = None
            for i, (l_, r_) in enumerate(pairs):
                bi = e.matmul(out, l_, r_, start=(i == 0), stop=(i == n - 1))
            return bi
        return self.op("pe", fn, r=r, w=w)

    def prologue(self):
        self.dma_sp(self.SMALL, self.small_d, w=[("SMALL",)])
        self.dma_sp(self.IDENT, self.ident_d, w=[("IDENT",)])
        self.dma_sp(self.DFTC, self.dftc_d, w=[("DFTC",)])
        self.op("dve", lambda e: e.memset(self.ONES, 1.0), w=[("ONES",)])
        for n_ in (1024, 512, 256):
            self.op("dve", (lambda n_: (lambda e: e.memset(self.INV[n_], 1.0 / n_)))(n_), w=[("INV", n_)])

    def sequence(self, si):
        for c in range(8):
            self.dma_sp(self.XT[:, c, :], self.xT[si, c * 128:(c + 1) * 128, :],
                        w=[("XT", c, t) for t in range(NT)])
        self.dma_sp(self.MEMT, self.memT[si].rearrange("(c p) m -> p c m", p=128), w=[("MEMT",)])
        self.norm_stats([(self.MEMT[:, c, :], ("MEMT",)) for c in range(8)], 1024, width=MEM,
                        out=self.MRS, outkey=("MRS",))
        phases = []
        for l in range(self.layers):
            if "ffn1" in self.parts:
                phases.append((("ffn1_norm", l), lambda nxt, l=l: self.ffn(l, 0, nxt)))
            if "mix" in self.parts:
                if l % 2 == 0:
                    phases.append((("mix_norm", l), lambda nxt, l=l: self.mixer_ab(l // 2, l, nxt)))
                else:
                    phases.append((("mix_norm", l), lambda nxt, l=l: self.mixer_mla(l // 2, l, nxt)))
            if "xattn" in self.parts:
                phases.append((("xattn_norm", l), lambda nxt, l=l: self.xattn(l, nxt)))
            if "ffn2" in self.parts:
                phases.append((("ffn2_norm", l), lambda nxt, l=l: self.ffn(l, 1, nxt)))
        self.pre_norm = None
        for n_, (gk, fn) in enumerate(phases):
            nxt = phases[n_ + 1][0] if n_ + 1 < len(phases) else None
            fn(nxt)
        self.final(si)

    def norm_stats(self, chunks, n, width=TT, out=None, outkey=None):
        b = self.bank()
        pst = self.ps[:, b, 0:width]
        nch = len(chunks)
        for c, (src, key) in enumerate(chunks):
            i = self.sq_i
            self.sq_i = (i + 1) % 4
            sq = self.SQ[:, i, 0:width]
            self.act(sq, src, AF.Square, r=[key], w=[("SQ", i)])
            self.op("pe", self._mm1(pst, self.INV[n], sq, c == 0, c == nch - 1),
                    r=[("SQ", i), ("INV", n)], w=[("ps", b)])
        j = self.rs_i
        self.rs_i = (j + 1) % 2
        rs = self.RS[:, j, 0:width]
        self.act(rs, pst, AF.Ln, r=[("ps", b)], w=[("RS", j)], bias=EPS, scale=1.0)
        if out is None:
            self.act(pst, rs, AF.Exp, r=[("RS", j)], w=[("ps", b)], scale=-0.5)
        else:
            self.act(out, rs, AF.Exp, r=[("RS", j)], w=[outkey], scale=-0.5)
        return b

    def recip_act(self, out, in_, r, w, width=TT):
        j = self.rs_i
        self.rs_i = (j + 1) % 2
        rs = self.RS[:, j, 0:width]
        self.act(rs, in_, AF.Ln, r=r, w=[("RS", j)])
        self.act(out, rs, AF.Exp, r=[("RS", j)], w=w, scale=-1.0)

    def norm_tile(self, gkey, t):
        ts = slice(t * TT, (t + 1) * TT)
        b = self.norm_stats([(self.XT[:, c, ts], ("XT", c, t)) for c in range(8)], 1024)
        for c in range(8):
            self.stt("dve", self.HT[:, c, ts], self.XT[:, c, ts], self.g(gkey, c), self.ps[:, b, :],
                     ALU.mult, ALU.mult, r=[("XT", c, t), ("ps", b), ("SMALL",)], w=[("HT", c, t)])

    def rmsnorm_x(self, gkey):
        if self.pre_norm == gkey:
            self.pre_norm = None
            return
        for t in range(NT):
            self.norm_tile(gkey, t)

    def stt(self, eng, out, in0, scalar, in1, op0, op1, r, w):
        return self.op(eng, lambda e: e.scalar_tensor_tensor(out=out, in0=in0, scalar=scalar, in1=in1, op0=op0, op1=op1), r=r, w=w)

    def tt(self, eng, out, in0, in1, op, r, w):
        return self.op(eng, lambda e: e.tensor_tensor(out=out, in0=in0, in1=in1, op=op), r=r, w=w)

    def ts_(self, eng, out, in0, s1, s2, op0, op1, r, w):
        return self.op(eng, lambda e: e.tensor_scalar(out=out, in0=in0, scalar1=s1, scalar2=s2, op0=op0, op1=op1), r=r, w=w)

    def act(self, out, in_, func, r, w, bias=None, scale=None):
        kw = {}
        if bias is not None:
            kw["bias"] = bias
        if scale is not None:
            kw["scale"] = scale
        return self.op("act", lambda e: e.activation(out=out, in_=in_, func=func, **kw), r=r, w=w)

    def copy(self, eng, out, in_, r, w):
        if eng == "act":
            return self.op("act", lambda e: e.activation(out=out, in_=in_, func=AF.Copy), r=r, w=w)
        return self.op(eng, lambda e: e.tensor_copy(out=out, in_=in_), r=r, w=w)

    def recip(self, out, in_, r, w):
        return self.op("dve", lambda e: e.reciprocal(out=out, in_=in_), r=r, w=w)

    def memset(self, eng, ap, val, w):
        return self.op(eng, lambda e: e.memset(ap, val), w=w)

    def ffn(self, l, which, nxt=None):
        S_ = self.S
        gkey = ("ffn1_norm" if which == 0 else "ffn2_norm", l)
        self.rmsnorm_x(gkey)
        groups = [[0, 1, 2, 3, 4, 5], [6, 7, 8, 9, 10, 11], [12, 13, 14, 15, 16], [17, 18, 19, 20, 21]]
        GMAX = 6
        NWGU = 4
        NWD = 12
        o = self.A0
        S_.region("ACTH", o, GMAX * S * 2)
        ACTH = self.view(o, [128, GMAX, S], BF16); o += GMAX * S * 2
        S_.region("WGU", o, NWGU * 4096)
        WGU = self.view(o, [128, NWGU, 2, 8, 128], BF16) if False else self.view(o, [128, NWGU, 2048], BF16); o += NWGU * 4096
        S_.region("WD", o, NWD * 2048)
        WD = self.view(o, [128, NWD, 1024], BF16); o += NWD * 2048
        S_.region("SG", o, 2 * TT * 4)
        SG = self.view(o, [128, 2, TT], F32); o += 2 * TT * 4
        assert o <= self.AEND
        wgu_i = 0
        wd_i = 0
        sg_i = 0
        for grp in groups:
            dslots = {}
            for fl, f in enumerate(grp):
                ws = wgu_i % NWGU
                wgu_i += 1
                self.dma_w(WGU[:, ws, :], self.ffn_wgu[l, which, f], [("WGU", ws)])
                ds = wd_i % NWD
                wd_i += 1
                dslots[f] = ds
                self.dma_w(WD[:, ds, :], self.ffn_wd[l, which, f * 128:(f + 1) * 128, :], [("WD", ds)])
                for t in range(NT):
                    ts = slice(t * TT, (t + 1) * TT)
                    bg = self.bank()
                    bu = self.bank()
                    hk = [("HT", k, t) for k in range(8)]
                    self.mm_group(self.ps[:, bg, :],
                                  [(WGU[:, ws, k * 128:(k + 1) * 128], self.HT[:, k, ts]) for k in range(8)],
                                  r=hk + [("WGU", ws)], w=[("ps", bg)])
                    self.mm_group(self.ps[:, bu, :],
                                  [(WGU[:, ws, 1024 + k * 128:1024 + (k + 1) * 128], self.HT[:, k, ts]) for k in range(8)],
                                  r=hk + [("WGU", ws)], w=[("ps", bu)])
                    sg = sg_i % 2
                    sg_i += 1
                    self.op("act", (lambda sg, bg: (lambda e: e.activation(out=SG[:, sg, :], in_=self.ps[:, bg, :], func=AF.Silu)))(sg, bg),
                            r=[("ps", bg)], w=[("SG", sg)])
                    self.op("dve", (lambda sg, bu, fl, ts: (lambda e: e.tensor_tensor(
                        out=ACTH[:, fl, ts], in0=SG[:, sg, :], in1=self.ps[:, bu, :], op=ALU.mult)))(sg, bu, fl, ts),
                        r=[("SG", sg), ("ps", bu)], w=[("ACTH", fl, t)])
            ng = len(grp)
            lastg = grp is groups[-1]
            for t in range(NT):
                ts = slice(t * TT, (t + 1) * TT)
                if lastg and nxt is not None and t >= 2:
                    self.norm_tile(nxt, t - 2)
                for d in range(8):
                    bd = self.bank()
                    self.mm_group(self.ps[:, bd, :],
                                  [(WD[:, dslots[f], d * 128:(d + 1) * 128], ACTH[:, fl, ts]) for fl, f in enumerate(grp)],
                                  r=[("ACTH", fl, t) for fl in range(ng)] + [("WD", dslots[f]) for f in grp],
                                  w=[("ps", bd)])
                    self.op("dve", (lambda d, ts, bd: (lambda e: e.scalar_tensor_tensor(
                        out=self.XT[:, d, ts], in0=self.ps[:, bd, :], scalar=0.5, in1=self.XT[:, d, ts],
                        op0=ALU.mult, op1=ALU.add)))(d, ts, bd),
                        r=[("ps", bd), ("XT", d, t)], w=[("XT", d, t)])
        if nxt is not None:
            self.norm_tile(nxt, NT - 2)
            self.norm_tile(nxt, NT - 1)
            self.pre_norm = nxt

    def final(self, si):
        S_ = self.S
        o = self.A0
        S_.region("YO", o, 2 * S * 4)
        YO = self.view(o, [128, 2, S], F32)
        gkey = ("final_norm", 0)
        banks = []
        for t in range(NT):
            ts = slice(t * TT, (t + 1) * TT)
            banks.append(self.norm_stats([(self.XT[:, c, ts], ("XT", c, t)) for c in range(8)], 1024))
        for c in range(8):
            yo = c % 2
            for t in range(NT):
                ts = slice(t * TT, (t + 1) * TT)
                b = banks[t]
                self.op("dve", (lambda c, ts, b, yo: (lambda e: e.scalar_tensor_tensor(
                    out=YO[:, yo, ts], in0=self.XT[:, c, ts], scalar=self.g(gkey, c), in1=self.ps[:, b, :],
                    op0=ALU.mult, op1=ALU.mult)))(c, ts, b, yo),
                    r=[("XT", c, t), ("ps", b), ("SMALL",)], w=[("YO", yo)])
            self.dma_sp(self.yT[si, c * 128:(c + 1) * 128, :], YO[:, yo, :], r=[("YO", yo)], w=[("OUT", si, c)])

    def wring(self, name, off, nslots):
        self.S.region(name, off, nslots * 2048)
        self._wr = (name, self.view(off, [128, nslots, 1024], BF16), nslots)
        self._wr_i = 0
        return off + nslots * 2048

    def wload(self, src):
        name, W, n = self._wr
        ws = self._wr_i % n
        self._wr_i += 1
        self.dma_w(W[:, ws, :], src, [(name, ws)])
        return W[:, ws, :], (name, ws)

    def evac_copy(self, out, in_, r, w, scale=None):
        self._ev = getattr(self, "_ev", 0) + 1
        if scale is not None:
            return self.op("act", lambda e: e.activation(out=out, in_=in_, func=AF.Copy, scale=scale), r=r, w=w)
        if self._ev % 2 == 0:
            return self.copy("act", out, in_, r, w)
        return self.copy("dve", out, in_, r, w)

    def xattn(self, l, nxt=None):
        S_ = self.S
        self.rmsnorm_x(("xattn_norm", l))
        o = self.A0
        S_.region("QT", o, 8 * S * 2); QT = self.view(o, [128, 8, S], BF16); o += 8 * S * 2
        S_.region("MNT", o, 8 * MEM * 2); MNT = self.view(o, [128, 8, MEM], BF16); o += 8 * MEM * 2
        S_.region("KT", o, 8 * MEM * 2); KT = self.view(o, [128, 8, MEM], BF16); o += 8 * MEM * 2
        S_.region("V", o, 2 * 1024 * 2); V = self.view(o, [128, 2, 1024], BF16); o += 2 * 1024 * 2
        oWV = o
        S_.region("WV", o, 8 * 1024 * 2); WV = self.view(o, [128, 8, 1024], BF16); o += 8 * 1024 * 2
        o = self.wring("W", o, 4)
        S_.region("PT", o, 4 * TT * 2); PT = self.view(o, [128, 4, TT], BF16); o += 4 * TT * 2
        S_.region("RD", o, 2 * TT * 4); RD = self.view(o, [128, 2, TT], F32); o += 2 * TT * 4
        assert o <= self.AEND
        for c in range(8):
            self.stt("dve", MNT[:, c, :], self.MEMT[:, c, :], self.g(("mem_norm", l), c), self.MRS,
                     ALU.mult, ALU.mult, r=[("MEMT",), ("MRS",), ("SMALL",)], w=[("MNT", c)])
        mk = [("MNT", c) for c in range(8)]
        self.dma_w(WV.rearrange("p k n -> p (k n)"), self.xa_wv[l], [("WV",)])
        for oc in range(8):
            W, wk = self.wload(self.xa_wq[l, oc])
            for t in range(NT):
                ts = slice(t * TT, (t + 1) * TT)
                b = self.bank()
                self.mm_group(self.ps[:, b, :], [(W[:, k * 128:(k + 1) * 128], self.HT[:, k, ts]) for k in range(8)],
                              r=[("HT", k, t) for k in range(8)] + [wk], w=[("ps", b)])
                self.evac_copy(QT[:, oc, ts], self.ps[:, b, :], r=[("ps", b)], w=[("QT", oc, t)], scale=1.0 / 16.0)
        for oc in range(8):
            W, wk = self.wload(self.xa_wk[l, oc])
            b = self.bank()
            self.mm_group(self.ps[:, b, 0:MEM], [(W[:, k * 128:(k + 1) * 128], MNT[:, k, :]) for k in range(8)],
                          r=mk + [wk], w=[("ps", b)])
            self.evac_copy(KT[:, oc, :], self.ps[:, b, 0:MEM], r=[("ps", b)], w=[("KT", oc)])
        for mt in range(2):
            for nh in range(2):
                b = self.bank()
                self.mm_group(self.ps[:, b, :],
                              [(MNT[:, k, mt * 128:(mt + 1) * 128], WV[:, k, nh * 512:(nh + 1) * 512]) for k in range(8)],
                              r=mk + [("WV",)], w=[("ps", b)])
                self.evac_copy(V[:, mt, nh * 512:(nh + 1) * 512], self.ps[:, b, :], r=[("ps", b)], w=[("V", mt, nh)])
        W8 = self.out_proj_load(oWV, lambda oc: self.xa_wo[l, oc])
        pt_i = 0
        rd_i = 0
        items = [(h, t) for h in range(4) for t in range(NT)]
        sbanks = [0, 1, 2, 3]
        sb_i = 0
        obanks = [4, 5, 6, 7]
        ob_i = 0
        pend = {}

        def scores(h, t):
            nonlocal sb_i, pt_i
            ts = slice(t * TT, (t + 1) * TT)
            slots = []
            for mt in range(2):
                bs = sbanks[sb_i % 4]; sb_i += 1
                self.mm_group(self.ps[:, bs, :],
                              [(KT[:, 2 * h + dc, mt * 128:(mt + 1) * 128], QT[:, 2 * h + dc, ts]) for dc in range(2)],
                              r=[("KT", 2 * h), ("KT", 2 * h + 1), ("QT", 2 * h, t), ("QT", 2 * h + 1, t)], w=[("ps", bs)])
                sl = pt_i % 4
                pt_i += 1
                self.act(PT[:, sl, :], self.ps[:, bs, :], AF.Exp, r=[("ps", bs)], w=[("PT", sl)])
                slots.append(sl)
            pend[(h, t)] = slots

        def pv(h, t):
            nonlocal ob_i, rd_i
            ts = slice(t * TT, (t + 1) * TT)
            slots = pend.pop((h, t))
            bd = obanks[ob_i % 4]; ob_i += 1
            self.mm_group(self.ps[:, bd, :], [(self.ONES, PT[:, sl, :]) for sl in slots],
                          r=[("PT", sl) for sl in slots] + [("ONES",)], w=[("ps", bd)])
            rj = rd_i % 2
            rd_i += 1
            self.recip_act(RD[:, rj, :], self.ps[:, bd, :], r=[("ps", bd)], w=[("RD", rj)])
            for dc in range(2):
                c = 2 * h + dc
                bo = obanks[ob_i % 4]; ob_i += 1
                self.mm_group(self.ps[:, bo, :],
                              [(V[:, mt, c * 128:(c + 1) * 128], PT[:, slots[mt], :]) for mt in range(2)],
                              r=[("PT", sl) for sl in slots] + [("V", mt, c // 4) for mt in range(2)], w=[("ps", bo)])
                self.tt("dve", self.HT[:, c, ts], self.ps[:, bo, :], RD[:, rj, :], ALU.mult,
                        r=[("ps", bo), ("RD", rj)], w=[("HT", c, t)])
        for n_, it in enumerate(items):
            scores(*it)
            if n_ >= 1:
                pv(*items[n_ - 1])
        pv(*items[-1])
        self.out_proj(W8, nxt)

    def out_proj_load(self, off, wsrc):
        self.S.region("W8", off, 8 * 2048)
        W8 = self.view(off, [128, 8, 1024], BF16)
        for oc in range(8):
            self.dma_w(W8[:, oc, :], wsrc(oc), [("W8", oc)])
        return W8

    def out_proj(self, W8, nxt=None):
        for t in range(NT):
            ts = slice(t * TT, (t + 1) * TT)
            for oc in range(8):
                b = self.bank()
                self.mm_group(self.ps[:, b, :], [(W8[:, oc, k * 128:(k + 1) * 128], self.HT[:, k, ts]) for k in range(8)],
                              r=[("HT", k, t) for k in range(8)] + [("W8", oc)], w=[("ps", b)])
                self.tt("dve", self.XT[:, oc, ts], self.ps[:, b, :], self.XT[:, oc, ts], ALU.add,
                        r=[("ps", b), ("XT", oc, t)], w=[("XT", oc, t)])
            if nxt is not None and t >= 1:
                self.norm_tile(nxt, t - 1)
        if nxt is not None:
            self.norm_tile(nxt, NT - 1)
            self.pre_norm = nxt

    def mixer_ab(self, i, l, nxt=None):
        S_ = self.S
        self.rmsnorm_x(("mix_norm", l))
        A0 = self.A0
        PADL = 16
        HPW = S + 32
        o = A0
        S_.region("UF", o, 4 * S * 2); UF = self.view(o, [128, 4, S], BF16); o += 4 * S * 2
        S_.region("HP", o, 4 * HPW * 2); HP = self.view(o, [128, 4, HPW], BF16); o += 4 * HPW * 2
        oB = o
        o = self.wring("W", o, 4)
        S_.region("SGT", o, 2 * TT * 4); SGT = self.view(o, [128, 2, TT], F32); o += 2 * TT * 4
        assert o <= self.AEND
        for ch in range(4):
            self.memset("dve", HP[:, ch, 0:PADL], 0.0, w=[("HP", ch, "padl")])
            self.memset("dve", HP[:, ch, PADL + S:HPW], 0.0, w=[("HP", ch, "padr")])
        for oc in range(4):
            W, wk = self.wload(self.ab_win[i, oc])
            for t in range(NT):
                ts = slice(t * TT, (t + 1) * TT)
                b = self.bank()
                self.mm_group(self.ps[:, b, :], [(W[:, k * 128:(k + 1) * 128], self.HT[:, k, ts]) for k in range(8)],
                              r=[("HT", k, t) for k in range(8)] + [wk], w=[("ps", b)])
                self.evac_copy(UF[:, oc, ts], self.ps[:, b, :], r=[("ps", b)], w=[("UF", oc, t)])
        sg_i = 0
        for ch in range(4):
            Wa, wka = self.wload(self.ab_win[i, 4 + ch])
            Wg, wkg = self.wload(self.ab_win[i, 8 + ch])
            for t in range(NT):
                ts = slice(t * TT, (t + 1) * TT)
                ba = self.bank()
                bg = self.bank()
                hk = [("HT", k, t) for k in range(8)]
                self.mm_group(self.ps[:, ba, :], [(Wa[:, k * 128:(k + 1) * 128], self.HT[:, k, ts]) for k in range(8)],
                              r=hk + [wka], w=[("ps", ba)])
                self.mm_group(self.ps[:, bg, :], [(Wg[:, k * 128:(k + 1) * 128], self.HT[:, k, ts]) for k in range(8)],
                              r=hk + [wkg], w=[("ps", bg)])
                sg = sg_i % 2
                sg_i += 1
                self.act(SGT[:, sg, :], self.ps[:, bg, :], AF.Sigmoid, r=[("ps", bg)], w=[("SGT", sg)])
                self.tt("dve", HP[:, ch, PADL + t * TT:PADL + (t + 1) * TT], SGT[:, sg, :], self.ps[:, ba, :], ALU.mult,
                        r=[("SGT", sg), ("ps", ba)], w=[("HP", ch, t)])
        o = oB
        S_.region("AB", o, 16 * 4 * 256 * 2); AB = self.view(o, [128, 16, 4, 256], BF16); o += 16 * 4 * 256 * 2
        NCS = 2
        S_.region("CS", o, NCS * 8 * 512 * 2); CS = self.view(o, [128, NCS, 8, 512], BF16); o += NCS * 8 * 512 * 2
        oDG0 = o
        S_.region("DG0", o, CONVW * 128 * 2); DG0 = self.view(o, [128, CONVW, 128], BF16); o += CONVW * 128 * 2
        assert o <= self.AEND

        def build_diag(DGv, key, ch):
            for j in range(CONVW):
                self.ts_("dve", DGv[:, j, :], self.IDENT, self.g(("ab_conv_w", i), j * 4 + ch), None, ALU.mult, ALU.bypass,
                         r=[("IDENT",), ("SMALL",)], w=[key])
        for tt_ in range(16):
            for gp in range(2):
                b = self.bank()
                for gg in range(2):
                    g_ = 2 * gp + gg
                    self.mm_group(self.ps[:, b, gg * 256:(gg + 1) * 256],
                                  [(UF[:, g_, tt_ * 128:(tt_ + 1) * 128], self.DFTC)],
                                  r=[("UF", g_, tt_ // 4), ("DFTC",)], w=[("ps", b)])
                self.evac_copy(AB[:, tt_, 2 * gp:2 * gp + 2, :].rearrange("p a b -> p (a b)"), self.ps[:, b, :],
                               r=[("ps", b)], w=[("AB", tt_, gp)])
        W8 = self.out_proj_load(A0, lambda oc: self.ab_wout[i, oc])
        build_diag(DG0, ("DG0",), 0)
        cs_i = 0
        for st in range(4):
            banks = [self.bank() for _ in range(4)]
            npiece = 4
            for pc in range(npiece):
                cs = pc // 2
                half = pc % 2
                sl = cs_i % NCS
                cs_i += 1
                self.dma_sp(CS[:, sl, :, :].rearrange("p a b -> p (a b)"),
                            self.dfts_d[st, cs, :, half * 8 * 512:(half + 1) * 8 * 512], w=[("CS", sl)])
                for g_ in range(4):
                    b = banks[g_]
                    pairs = [(AB[:, half * 8 + s8, g_, cs * 128:(cs + 1) * 128], CS[:, sl, s8, :]) for s8 in range(8)]
                    self.mm_acc(self.ps[:, b, :], pairs, first=(pc == 0), last=(pc == npiece - 1),
                                r=[("AB", half * 8 + s8, g_ // 2) for s8 in range(8)] + [("CS", sl)], w=[("ps", b)])
            for g_ in range(4):
                self.evac_copy(self.HT[:, g_, st * TT:(st + 1) * TT], self.ps[:, banks[g_], :],
                               r=[("ps", banks[g_])], w=[("HT", g_, st)])
        o = oB
        S_.region("DG1", o, CONVW * 128 * 2); DG1 = self.view(o, [128, CONVW, 128], BF16); o += CONVW * 128 * 2
        S_.region("CO", o, 4 * S * 4); CO = self.view(o, [128, 4, S], F32); o += 4 * S * 4
        assert o <= oDG0
        for ch in range(4):
            DGv, dkey = (DG0, ("DG0",)) if ch % 2 == 0 else (DG1, ("DG1",))
            if ch > 0:
                build_diag(DGv, dkey, ch)
            for t in range(NT):
                ts = slice(t * TT, (t + 1) * TT)
                b = self.bank()
                self.mm_group(self.ps[:, b, :],
                              [(DGv[:, j, :], HP[:, ch, t * TT + j + 1:t * TT + j + 1 + TT]) for j in range(CONVW)],
                              r=[dkey, ("HP", ch, "padl"), ("HP", ch, "padr")] + [("HP", ch, tq) for tq in range(NT)],
                              w=[("ps", b)])
                self.act(CO[:, ch, ts], self.ps[:, b, :], AF.Identity, r=[("ps", b), ("SMALL",)], w=[("CO", ch, t)],
                         bias=self.g(("ab_conv_b", i), ch))
        self.ps_rr = 0
        lnb = []
        for t in range(NT):
            ts = slice(t * TT, (t + 1) * TT)
            bm = self.bank()
            bq = self.bank()
            lnb.append((bm, bq))
            for ch in range(4):
                i1 = self.sq_i; self.sq_i = (i1 + 1) % 4
                self.act(self.SQ[:, i1, :], CO[:, ch, ts], AF.Square, r=[("CO", ch, t)], w=[("SQ", i1)])
                self.op("pe", self._mm1(self.ps[:, bq, :], self.INV[512], self.SQ[:, i1, :], ch == 0, ch == 3),
                        r=[("SQ", i1), ("INV", 512)], w=[("ps", bq)])
                i2 = self.sq_i; self.sq_i = (i2 + 1) % 4
                self.copy("dve", self.SQ[:, i2, :], CO[:, ch, ts], r=[("CO", ch, t)], w=[("SQ", i2)])
                self.op("pe", self._mm1(self.ps[:, bm, :], self.INV[512], self.SQ[:, i2, :], ch == 0, ch == 3),
                        r=[("SQ", i2), ("INV", 512)], w=[("ps", bm)])
        for t in range(NT):
            bm, bq = lnb[t]
            j = self.rs_i; self.rs_i = (j + 1) % 2
            rs = self.RS[:, j, :]
            self.act(rs, self.ps[:, bm, :], AF.Square, r=[("ps", bm)], w=[("RS", j)])
            self.tt("dve", rs, self.ps[:, bq, :], rs, ALU.subtract, r=[("ps", bq), ("RS", j)], w=[("RS", j)])
            self.act(rs, rs, AF.Ln, r=[("RS", j)], w=[("RS", j)], bias=EPS, scale=1.0)
            self.act(self.ps[:, bq, :], rs, AF.Exp, r=[("RS", j)], w=[("ps", bq)], scale=-0.5)
        for t in range(NT):
            ts = slice(t * TT, (t + 1) * TT)
            bm, bq = lnb[t]
            for ch in range(4):
                self.tt("dve", CO[:, ch, ts], CO[:, ch, ts], self.ps[:, bm, :], ALU.subtract,
                        r=[("CO", ch, t), ("ps", bm)], w=[("CO", ch, t)])
                self.tt("dve", CO[:, ch, ts], CO[:, ch, ts], self.ps[:, bq, :], ALU.mult,
                        r=[("CO", ch, t), ("ps", bq)], w=[("CO", ch, t)])
                self.act(self.HT[:, 4 + ch, ts], CO[:, ch, ts], AF.Silu, r=[("CO", ch, t), ("SMALL",)], w=[("HT", 4 + ch, t)],
                         bias=self.g(("ab_conv_ln_b", i), ch), scale=self.g(("ab_conv_ln_g", i), ch))
        self.out_proj(W8, nxt)

    def _mm1(self, out, lhsT, rhs, start, stop):
        return lambda e: e.matmul(out, lhsT, rhs, start=start, stop=stop)

    def mm_acc(self, out, pairs, first, last, r, w):
        n = len(pairs)

        def fn(e):
            bi = None
            for i_, (l_, r_) in enumerate(pairs):
                bi = e.matmul(out, l_, r_, start=(first and i_ == 0), stop=(last and i_ == n - 1))
            return bi
        return self.op("pe", fn, r=r, w=w)

    def mixer_mla(self, i, l, nxt=None):
        S_ = self.S
        self.rmsnorm_x(("mix_norm", l))
        o = self.A0
        oCQN = o
        S_.region("CQN", o, 4 * S * 2); CQN = self.view(o, [128, 4, S], BF16); o += 4 * S * 2
        S_.region("CKVN", o, 2 * S * 2); CKVN = self.view(o, [128, 2, S], BF16); o += 2 * S * 2
        S_.region("KR", o, S * 2); KR = self.view(o, [128, S], BF16); o += S * 2
        S_.region("ROPE", o, 2 * S * 4); ROPE = self.view(o, [128, 2, S], F32); o += 2 * S * 4
        S_.region("T12", o, 2 * TT * 4); T12 = self.view(o, [128, 2, TT], F32); o += 2 * TT * 4
        oP = o
        S_.region("WIN", o, 8 * 2048); WIN = self.view(o, [128, 8, 1024], BF16); o += 8 * 2048
        S_.region("CQ32", o, 4 * TT * 4); CQ32 = self.view(o, [128, 4, TT], F32); o += 4 * TT * 4
        S_.region("CKV32", o, 2 * TT * 4); CKV32 = self.view(o, [128, 2, TT], F32); o += 2 * TT * 4
        assert o <= self.AEND
        self.dma_sp(ROPE[:, 0, :], self.rope_d[0], w=[("ROPE", 0)])
        self.dma_sp(ROPE[:, 1, :], self.rope_d[1], w=[("ROPE", 1)])
        for oc in range(8):
            self.dma_w(WIN[:, oc, :], self.mla_win[i, oc], [("WIN", oc)])

        def rope_combine(dst, ba, bb, ts, wkey):
            self.tt("dve", T12[:, 0, :], self.ps[:, ba, :], ROPE[:, 0, ts], ALU.mult,
                    r=[("ps", ba), ("ROPE", 0)], w=[("T12", 0)])
            self.tt("dve", T12[:, 1, :], self.ps[:, bb, :], ROPE[:, 1, ts], ALU.mult,
                    r=[("ps", bb), ("ROPE", 1)], w=[("T12", 1)])
            self.tt("dve", dst, T12[:, 0, :], T12[:, 1, :], ALU.add,
                    r=[("T12", 0), ("T12", 1)], w=[wkey])

        for t in range(NT):
            ts = slice(t * TT, (t + 1) * TT)
            hk = [("HT", k, t) for k in range(8)]

            def proj(oc):
                b = self.bank()
                self.mm_group(self.ps[:, b, :], [(WIN[:, oc, k * 128:(k + 1) * 128], self.HT[:, k, ts]) for k in range(8)],
                              r=hk + [("WIN", oc)], w=[("ps", b)])
                return b
            for oc in range(4):
                b = proj(oc)
                self.copy("act", CQ32[:, oc, :], self.ps[:, b, :], r=[("ps", b)], w=[("CQ32", oc)])
            b = self.norm_stats([(CQ32[:, oc, :], ("CQ32", oc)) for oc in range(4)], 512)
            for oc in range(4):
                self.stt("dve", CQN[:, oc, ts], CQ32[:, oc, :], self.g(("mla_q_norm", i), oc), self.ps[:, b, :],
                         ALU.mult, ALU.mult, r=[("CQ32", oc), ("ps", b), ("SMALL",)], w=[("CQN", oc, t)])
            for oc in range(2):
                b = proj(4 + oc)
                self.copy("act", CKV32[:, oc, :], self.ps[:, b, :], r=[("ps", b)], w=[("CKV32", oc)])
            b = self.norm_stats([(CKV32[:, oc, :], ("CKV32", oc)) for oc in range(2)], 256)
            for oc in range(2):
                self.stt("dve", CKVN[:, oc, ts], CKV32[:, oc, :], self.g(("mla_kv_norm", i), oc), self.ps[:, b, :],
                         ALU.mult, ALU.mult, r=[("CKV32", oc), ("ps", b), ("SMALL",)], w=[("CKVN", oc, t)])
            ba = proj(6)
            bb = proj(7)
            rope_combine(KR[:, ts], ba, bb, ts, ("KR", t))

        o = oP
        S_.region("QN", o, S * 2); QN = self.view(o, [128, S], BF16); o += S * 2
        S_.region("QR", o, S * 2); QR = self.view(o, [128, S], BF16); o += S * 2
        S_.region("KN", o, S * 2); KN = self.view(o, [128, S], BF16); o += S * 2
        S_.region("VH", o, S * 2); VH = self.view(o, [128, 16, 128], BF16); o += S * 2
        S_.region("WQ", o, 2 * 3072); WQ = self.view(o, [128, 2, 1536], BF16); o += 2 * 3072
        S_.region("WKV", o, 2 * 1024); WKV = self.view(o, [128, 2, 512], BF16); o += 2 * 1024
        NPT = 6
        S_.region("PT", o, NPT * TT * 2); PT = self.view(o, [128, NPT, TT], BF16); o += NPT * TT * 2
        S_.region("RD", o, 2 * TT * 4); RD = self.view(o, [128, 2, TT], F32); o += 2 * TT * 4
        NDS_ = 4
        S_.region("DS", o, NDS_ * TT * 2); DS = self.view(o, [128, NDS_, TT], BF16); o += NDS_ * TT * 2
        ds_i = 0
        oW = o
        assert o <= self.AEND, (o, self.AEND)
        SCALE = 1.0 / math.sqrt(192.0)
        pj = [7, 0, 1]
        pj_i = 0
        sb = [0, 1, 2]
        sb_i = 0
        DO = [(3, 4), (5, 6)]
        do_i = 0
        pt_i = 0
        rd_i = 0
        LA = 2
        for h in range(8):
            s = h % 2
            self.dma_w(WQ[:, s, :], self.mla_wq[i, h], [("WQ", s)])
            self.dma_w(WKV[:, s, :], self.mla_wkv[i, h], [("WKV", s)])
            for t in range(NT):
                ts = slice(t * TT, (t + 1) * TT)
                cq = [("CQN", k, t) for k in range(4)]
                b = pj[pj_i % 3]; pj_i += 1
                self.mm_group(self.ps[:, b, :], [(WQ[:, s, k * 384:k * 384 + 128], CQN[:, k, ts]) for k in range(4)],
                              r=cq + [("WQ", s)], w=[("ps", b)])
                self.copy("act", QN[:, ts], self.ps[:, b, :], r=[("ps", b)], w=[("QN", t)])
                ba = pj[pj_i % 3]; pj_i += 1
                self.mm_group(self.ps[:, ba, :], [(WQ[:, s, k * 384 + 128:k * 384 + 256], CQN[:, k, ts]) for k in range(4)],
                              r=cq + [("WQ", s)], w=[("ps", ba)])
                bb = pj[pj_i % 3]; pj_i += 1
                self.mm_group(self.ps[:, bb, :], [(WQ[:, s, k * 384 + 256:k * 384 + 384], CQN[:, k, ts]) for k in range(4)],
                              r=cq + [("WQ", s)], w=[("ps", bb)])
                rope_combine(QR[:, ts], ba, bb, ts, ("QR", t))
                b = pj[pj_i % 3]; pj_i += 1
                self.mm_group(self.ps[:, b, :], [(WKV[:, s, k * 256:k * 256 + 128], CKVN[:, k, ts]) for k in range(2)],
                              r=[("CKVN", k, t) for k in range(2)] + [("WKV", s)], w=[("ps", b)])
                self.copy("act", KN[:, ts], self.ps[:, b, :], r=[("ps", b)], w=[("KN", t)])
            for kt4 in range(4):
                b = pj[pj_i % 3]; pj_i += 1
                for q in range(4):
                    kt = kt4 * 4 + q
                    self.mm_group(self.ps[:, b, q * 128:(q + 1) * 128],
                                  [(CKVN[:, k, kt * 128:(kt + 1) * 128], WKV[:, s, k * 256 + 128:k * 256 + 256]) for k in range(2)],
                                  r=[("CKVN", k, kt4) for k in range(2)] + [("WKV", s)], w=[("ps", b)])
                self.copy("dve", VH[:, kt4 * 4:(kt4 + 1) * 4, :].rearrange("p a b -> p (a b)"), self.ps[:, b, :],
                          r=[("ps", b)], w=[("VH", kt4)])
            if h == 7:
                W8 = self.out_proj_load(oCQN, lambda oc: self.mla_wout[i, oc])
            items = [(t, kt) for t in range(NT) for kt in range(16)]
            slots = {}
            pend = []
            DLAG = 2

            def den_fn(BD, BO, dA, kt, t, h):
                def fn():
                    nonlocal rd_i
                    self.mm_acc(self.ps[:, BD, :], [(self.ONES, DS[:, dA, :])], first=(kt == 3), last=(kt == 15),
                                r=[("DS", dA), ("ONES",)], w=[("ps", BD)])
                    if kt == 15:
                        ts_ = slice(t * TT, (t + 1) * TT)
                        rj = rd_i % 2; rd_i += 1
                        self.recip_act(RD[:, rj, :], self.ps[:, BD, :], r=[("ps", BD)], w=[("RD", rj)])
                        self.tt("dve", self.HT[:, h, ts_], self.ps[:, BO, :], RD[:, rj, :], ALU.mult,
                                r=[("ps", BO), ("RD", rj)], w=[("HT", h, t)])
                return fn
            prev_sl = None
            dA = None
            for n_ in range(len(items) + LA + DLAG):
                if n_ < len(items):
                    t, kt = items[n_]
                    ts = slice(t * TT, (t + 1) * TT)
                    bs = sb[sb_i % 3]; sb_i += 1
                    self.mm_group(self.ps[:, bs, :],
                                  [(KN[:, kt * 128:(kt + 1) * 128], QN[:, ts]), (KR[:, kt * 128:(kt + 1) * 128], QR[:, ts])],
                                  r=[("KN", kt // 4), ("QN", t), ("KR", kt // 4), ("QR", t)], w=[("ps", bs)])
                    sl = pt_i % NPT; pt_i += 1
                    slots[(t, kt)] = sl
                    self.act(PT[:, sl, :], self.ps[:, bs, :], AF.Exp, r=[("ps", bs)], w=[("PT", sl)], scale=SCALE)
                if LA <= n_ < len(items) + LA:
                    t, kt = items[n_ - LA]
                    sl = slots.pop((t, kt))
                    BD, BO = DO[do_i % 2]
                    self.mm_acc(self.ps[:, BO, :], [(VH[:, kt, :], PT[:, sl, :])], first=(kt == 0), last=(kt == 15),
                                r=[("PT", sl), ("VH", kt // 4)], w=[("ps", BO)])
                    q = kt % 4
                    if q == 1 or q == 3:
                        d = ds_i % NDS_; ds_i += 1
                        self.tt("dve", DS[:, d, :], PT[:, prev_sl, :], PT[:, sl, :], ALU.add,
                                r=[("PT", prev_sl), ("PT", sl)], w=[("DS", d)])
                        if q == 1:
                            dA = d
                        else:
                            self.tt("dve", DS[:, dA, :], DS[:, dA, :], DS[:, d, :], ALU.add,
                                    r=[("DS", dA), ("DS", d)], w=[("DS", dA)])
                            pend.append((n_ + DLAG, den_fn(BD, BO, dA, kt, t, h)))
                    prev_sl = sl
                    if kt == 15:
                        do_i += 1
                while pend and pend[0][0] <= n_:
                    pend.pop(0)[1]()
            assert not pend
        self.ps_rr = 0
        self.out_proj(W8, nxt)


def _run(inputs, nseq=SEQ_PER_CORE, layers=DEPTH, parts=("ffn1", "mix", "xattn", "ffn2"), ncores=NCORES, trace=False):
    xs = np.concatenate([np.asarray(inputs["x_prompt"], np.float32), np.asarray(inputs["x_sample"], np.float32)], axis=0)
    ms = np.concatenate([np.asarray(inputs["mem_prompt"], np.float32), np.asarray(inputs["mem_sample"], np.float32)], axis=0)
    w = prep_weights(inputs)
    w.update(make_consts())
    b = Builder(nseq=nseq, layers=layers, parts=parts)
    nc = b.build()
    in_maps = []
    for c in range(ncores):
        sl = slice(c * nseq, (c + 1) * nseq)
        m = dict(w)
        m["xT"] = np.ascontiguousarray(xs[sl].transpose(0, 2, 1))
        m["memT"] = np.ascontiguousarray(ms[sl].transpose(0, 2, 1))
        in_maps.append(m)
    res = run_bass_kernel_spmd(nc, in_maps, core_ids=list(range(ncores)), trace=trace)
    ys = np.concatenate([np.asarray(r["yT"]).transpose(0, 2, 1) for r in res.results], axis=0)
    return np.ascontiguousarray(ys.astype(np.float32)), res


def kernel(**inputs):
    ys, _ = _run(inputs)
    nb = inputs["x_prompt"].shape[0]
    return (ys[:nb], ys[nb:])
```
